# Optimizing a Trainium2 kernel written in Bass

```python
import math
import jax, jax.numpy as jnp
from jax import lax
import numpy as np

D_MODEL = 1024
BATCH = 32
SEQ = 256
DEPTH = 2
DEC_BATCH = 2
DEC_SEQ = 1024
PAST_LEN = 256

GRID_W = 64
N_MIXERS = 2
N_GLA = (DEPTH + 1) // 2
N_ATTN = DEPTH // 2
GLA_HEADS = 4
GLA_DK = D_MODEL // 2 // GLA_HEADS
GLA_DV = D_MODEL // GLA_HEADS
GLA_KD = GLA_HEADS * GLA_DK
GLA_VD = GLA_HEADS * GLA_DV
GLA_GATE_RANK = 16
GLA_TAU = 16.0
GLA_CHUNK = 64
ATTN_HEADS = 8
ATTN_KV_HEADS = 2
HEAD_DIM = D_MODEL // ATTN_HEADS
ATTN_QD = ATTN_HEADS * HEAD_DIM
ATTN_KVD = ATTN_KV_HEADS * HEAD_DIM
ROPE_AXIS_DIM = HEAD_DIM // 2
ROPE_THETA = 10000.0
Q_BLOCK = 128
N_EXPERTS = 16
EC_CAPACITY_FACTOR = 2
D_FF_EXPERT = 2 * D_MODEL
ALPHA = (2 * DEPTH) ** 0.25
BETA = (8 * DEPTH) ** -0.25
NORM_EPS = 1e-6

kernel_name = "hybrid_gla_gqa_ec_moe_diffusion_step"


def layer_norm(x, g, b):
    xf = x.astype(jnp.float32)
    mu = jnp.mean(xf, axis=-1, keepdims=True)
    var = jnp.mean(jnp.square(xf - mu), axis=-1, keepdims=True)
    return ((xf - mu) * lax.rsqrt(var + NORM_EPS) * g + b).astype(x.dtype)


def rms_norm(x, g):
    xf = x.astype(jnp.float32)
    return (xf * lax.rsqrt(jnp.mean(jnp.square(xf), axis=-1, keepdims=True) + NORM_EPS) * g).astype(x.dtype)


def adaln_params(cvec, w_mod, b_mod):
    m = (jax.nn.silu(cvec) @ w_mod + b_mod)[:, None, :]
    return jnp.split(m, 6, axis=-1)


def split_heads(x, n):
    B, T, _ = x.shape
    return x.reshape(B, T, n, -1).transpose(0, 2, 1, 3)


def gla_chunk_scan(q, k, v, logg, S0):
    B, H, T, DK = q.shape
    DV = v.shape[-1]
    n = T // GLA_CHUNK
    rs = lambda a: a.reshape(B, H, n, GLA_CHUNK, a.shape[-1])
    q, k, v, logg = rs(q), rs(k), rs(v), rs(logg)
    G = jnp.cumsum(logg, axis=3)
    G_last = G[:, :, :, -1:, :]
    qg = q * jnp.exp(G)
    kg = k * jnp.exp(-G)
    mask = jnp.tril(jnp.ones((GLA_CHUNK, GLA_CHUNK), dtype=bool))
    A = jnp.where(mask, jnp.einsum('bhncd,bhnsd->bhncs', qg, kg), 0.0)
    o_intra = jnp.einsum('bhncs,bhnse->bhnce', A, v)
    dS = jnp.einsum('bhncd,bhnce->bhnde', k * jnp.exp(G_last - G), v)
    decay = jnp.exp(G_last[:, :, :, 0, :])

    def step(S, xs):
        qg_c, decay_c, dS_c = xs
        o_c = jnp.einsum('bhcd,bhde->bhce', qg_c, S)
        return decay_c[..., None] * S + dS_c, o_c

    S_fin, o_inter = lax.scan(step, S0.astype(jnp.float32),
                              (jnp.moveaxis(qg, 2, 0), jnp.moveaxis(decay, 2, 0), jnp.moveaxis(dS, 2, 0)))
    o = o_intra + jnp.moveaxis(o_inter, 0, 2)
    return o.reshape(B, H, T, DV), S_fin


def gla_mixer(h, S0_f, S0_b, w_in, w_gf1, w_gf2, b_gf, w_gb1, w_gb2, b_gb, g_norm, w_out):
    B, T, _ = h.shape
    proj = h @ w_in
    q, k, v, r = jnp.split(proj, [GLA_KD, 2 * GLA_KD, 2 * GLA_KD + GLA_VD], axis=-1)
    f32 = lambda a, n: split_heads(a, n).astype(jnp.float32)
    q = f32(q, GLA_HEADS) * GLA_DK ** -0.5
    k = f32(k, GLA_HEADS)
    v = f32(v, GLA_HEADS)

    def log_gate(w1, w2, b):
        z = ((h @ w1) @ w2 + b).astype(jnp.float32)
        return split_heads(jax.nn.log_sigmoid(z) / GLA_TAU, GLA_HEADS)

    flip = lambda a: a[:, :, ::-1]
    o_f, S_f = gla_chunk_scan(q, k, v, log_gate(w_gf1, w_gf2, b_gf), S0_f)
    o_b, S_b = gla_chunk_scan(flip(q), flip(k), flip(v), flip(log_gate(w_gb1, w_gb2, b_gb)), S0_b)
    o = (o_f + flip(o_b)).transpose(0, 2, 1, 3)
    o = rms_norm(o, g_norm.reshape(GLA_HEADS, GLA_DV))
    o = o.reshape(B, T, GLA_VD).astype(h.dtype) * jax.nn.silu(r)
    return o @ w_out, S_f, S_b


def axial_rope_tables(T):
    rows = T // GRID_W
    row = jnp.repeat(jnp.arange(rows), GRID_W).astype(jnp.float32)
    col = jnp.tile(jnp.arange(GRID_W), rows).astype(jnp.float32)
    inv = ROPE_THETA ** (-jnp.arange(0, ROPE_AXIS_DIM, 2, dtype=jnp.float32) / ROPE_AXIS_DIM)
    ang = jnp.stack([row[:, None] * inv, col[:, None] * inv], axis=1)
    return jnp.cos(ang), jnp.sin(ang)


def apply_axial_rope(x, cos, sin):
    B, H, T, _ = x.shape
    xs = x.astype(jnp.float32).reshape(B, H, T, 2, 2, ROPE_AXIS_DIM // 2)
    x1, x2 = xs[..., 0, :], xs[..., 1, :]
    out = jnp.stack([x1 * cos - x2 * sin, x2 * cos + x1 * sin], axis=-2)
    return out.reshape(B, H, T, HEAD_DIM).astype(x.dtype)


def block_attention(q, k, v):
    B, Hq, T, hd = q.shape
    Hkv = k.shape[1]
    G = Hq // Hkv
    nb = T // Q_BLOCK
    qb = q.reshape(B, Hkv, G, nb, Q_BLOCK, hd).transpose(3, 0, 1, 2, 4, 5)

    def one_block(qblk):
        s = jnp.einsum('bkgqd,bksd->bkgqs', qblk, k).astype(jnp.float32) * hd ** -0.5
        p = jax.nn.softmax(s, axis=-1).astype(v.dtype)
        return jnp.einsum('bkgqs,bksd->bkgqd', p, v)

    o = lax.map(one_block, qb)
    return o.transpose(1, 2, 3, 0, 4, 5).reshape(B, Hq, T, hd)


def attn_mixer(h, ctx_k, ctx_v, w_in, g_q, g_k, w_out):
    B, T, _ = h.shape
    q, k, v = jnp.split(h @ w_in, [ATTN_QD, ATTN_QD + ATTN_KVD], axis=-1)
    q = rms_norm(split_heads(q, ATTN_HEADS), g_q)
    k = rms_norm(split_heads(k, ATTN_KV_HEADS), g_k)
    v = split_heads(v, ATTN_KV_HEADS)
    if ctx_k is None:
        keys, vals = k, v
    else:
        cos, sin = axial_rope_tables(T)
        q = apply_axial_rope(q, cos, sin)
        k = apply_axial_rope(k, cos, sin)
        keys = jnp.concatenate([ctx_k.astype(k.dtype), k], axis=2)
        vals = jnp.concatenate([ctx_v.astype(v.dtype), v], axis=2)
    o = block_attention(q, keys, vals).transpose(0, 2, 1, 3).reshape(B, T, ATTN_QD)
    return o @ w_out, k, v


def expert_choice_ffn(h, w_router, w_gate, w_up, w_down):
    B, T, D = h.shape
    N = B * T
    C = EC_CAPACITY_FACTOR * N // N_EXPERTS
    xt = h.reshape(N, D)
    aff = jax.nn.softmax((xt @ w_router).astype(jnp.float32), axis=-1)
    g, idx = lax.top_k(aff.T, C)
    xe = xt[idx]
    hid = jax.nn.silu(jnp.einsum('ecd,edf->ecf', xe, w_gate)) * jnp.einsum('ecd,edf->ecf', xe, w_up)
    ye = jnp.einsum('ecf,efd->ecd', hid, w_down) * g[..., None].astype(h.dtype)
    out = jnp.zeros_like(xt).at[idx.reshape(-1)].add(ye.reshape(-1, D).astype(xt.dtype))
    return out.reshape(B, T, D)


def setup_inputs(seed: int = 0) -> dict:
    key = jax.random.key(seed)
    ks = iter(jax.random.split(key, 40))
    nrm = lambda shape, s: jax.random.normal(next(ks), shape, jnp.float32) * s
    D = D_MODEL
    return {
        "x_prompt": nrm((BATCH, SEQ, D), 1.0),
        "x_sample": nrm((DEC_BATCH, DEC_SEQ, D), 1.0),
        "state_gla_fwd": nrm((DEC_BATCH, N_GLA, GLA_HEADS, GLA_DK, GLA_DV), 0.5),
        "state_gla_bwd": nrm((DEC_BATCH, N_GLA, GLA_HEADS, GLA_DK, GLA_DV), 0.5),
        "cache_attn_k": nrm((DEC_BATCH, N_ATTN, ATTN_KV_HEADS, PAST_LEN, HEAD_DIM), 1.0),
        "cache_attn_v": nrm((DEC_BATCH, N_ATTN, ATTN_KV_HEADS, PAST_LEN, HEAD_DIM), 1.0),
        "c": nrm((DEC_BATCH, D), 1.0),
        "c_ctx": nrm((D,), 1.0),
        "w_mod": nrm((DEPTH, D, 6 * D), 0.5 * D ** -0.5),
        "b_mod": nrm((DEPTH, 6 * D), 0.02),
        "ln_g": 1.0 + nrm((DEPTH, 2, D), 0.02),
        "ln_b": nrm((DEPTH, 2, D), 0.02),
        "w_gla_in": nrm((N_GLA, D, 2 * GLA_KD + 2 * GLA_VD), D ** -0.5),
        "w_gla_gf1": nrm((N_GLA, D, GLA_GATE_RANK), D ** -0.5),
        "w_gla_gf2": nrm((N_GLA, GLA_GATE_RANK, GLA_KD), GLA_GATE_RANK ** -0.5),
        "b_gla_gf": nrm((N_GLA, GLA_KD), 0.1),
        "w_gla_gb1": nrm((N_GLA, D, GLA_GATE_RANK), D ** -0.5),
        "w_gla_gb2": nrm((N_GLA, GLA_GATE_RANK, GLA_KD), GLA_GATE_RANK ** -0.5),
        "b_gla_gb": nrm((N_GLA, GLA_KD), 0.1),
        "g_gla_norm": 1.0 + nrm((N_GLA, GLA_VD), 0.02),
        "w_gla_out": nrm((N_GLA, GLA_VD, D), BETA * GLA_VD ** -0.5),
        "w_attn_in": nrm((N_ATTN, D, ATTN_QD + 2 * ATTN_KVD), D ** -0.5),
        "g_attn_q": 1.0 + nrm((N_ATTN, HEAD_DIM), 0.02),
        "g_attn_k": 1.0 + nrm((N_ATTN, HEAD_DIM), 0.02),
        "w_attn_out": nrm((N_ATTN, ATTN_QD, D), BETA * ATTN_QD ** -0.5),
        "w_router": nrm((DEPTH, D, N_EXPERTS), D ** -0.5),
        "w_moe_gate": nrm((DEPTH, N_EXPERTS, D, D_FF_EXPERT), D ** -0.5),
        "w_moe_up": nrm((DEPTH, N_EXPERTS, D, D_FF_EXPERT), D ** -0.5),
        "w_moe_down": nrm((DEPTH, N_EXPERTS, D_FF_EXPERT, D), BETA * D_FF_EXPERT ** -0.5),
    }


def reference(x_prompt, x_sample, state_gla_fwd, state_gla_bwd, cache_attn_k, cache_attn_v, c, c_ctx,
              w_mod, b_mod, ln_g, ln_b,
              w_gla_in, w_gla_gf1, w_gla_gf2, b_gla_gf, w_gla_gb1, w_gla_gb2, b_gla_gb, g_gla_norm, w_gla_out,
              w_attn_in, g_attn_q, g_attn_k, w_attn_out,
              w_router, w_moe_gate, w_moe_up, w_moe_down):
    xp, xs = x_prompt, x_sample
    Bp = xp.shape[0]
    new_f, new_b, new_k, new_v = [], [], [], []
    for i in range(DEPTH):
        j = i // N_MIXERS
        sh1_p, sc1_p, ga1_p, sh2_p, sc2_p, ga2_p = adaln_params(c_ctx[None, :], w_mod[i], b_mod[i])
        sh1_s, sc1_s, ga1_s, sh2_s, sc2_s, ga2_s = adaln_params(c, w_mod[i], b_mod[i])
        hp = xp * (1 + sc1_p) + sh1_p
        hs = xs * (1 + sc1_s) + sh1_s
        if i % N_MIXERS == 0:
            gla_w = (w_gla_in[j], w_gla_gf1[j], w_gla_gf2[j], b_gla_gf[j], w_gla_gb1[j], w_gla_gb2[j],
                     b_gla_gb[j], g_gla_norm[j], w_gla_out[j])
            zero_state = jnp.zeros((Bp, GLA_HEADS, GLA_DK, GLA_DV), jnp.float32)
            mp, s_f, s_b = gla_mixer(hp, zero_state, zero_state, *gla_w)
            ms, _, _ = gla_mixer(hs, state_gla_fwd[:, j], state_gla_bwd[:, j], *gla_w)
            new_f.append(s_f)
            new_b.append(s_b)
        else:
            attn_w = (w_attn_in[j], g_attn_q[j], g_attn_k[j], w_attn_out[j])
            mp, k_ctx, v_ctx = attn_mixer(hp, None, None, *attn_w)
            ms, _, _ = attn_mixer(hs, cache_attn_k[:, j], cache_attn_v[:, j], *attn_w)
            new_k.append(k_ctx)
            new_v.append(v_ctx)
        xp = layer_norm(ALPHA * xp + ga1_p * mp, ln_g[i, 0], ln_b[i, 0])
        xs = layer_norm(ALPHA * xs + ga1_s * ms, ln_g[i, 0], ln_b[i, 0])
        moe_w = (w_router[i], w_moe_gate[i], w_moe_up[i], w_moe_down[i])
        fp = expert_choice_ffn(xp * (1 + sc2_p) + sh2_p, *moe_w)
        fs = expert_choice_ffn(xs * (1 + sc2_s) + sh2_s, *moe_w)
        xp = layer_norm(ALPHA * xp + ga2_p * fp, ln_g[i, 1], ln_b[i, 1])
        xs = layer_norm(ALPHA * xs + ga2_s * fs, ln_g[i, 1], ln_b[i, 1])
    return (xp, xs, jnp.stack(new_f, axis=1), jnp.stack(new_b, axis=1),
            jnp.stack(new_k, axis=1), jnp.stack(new_v, axis=1))
```

```python
import numpy as np
import ml_dtypes
import concourse.bass as bass
import concourse.mybir as mybir
from concourse.bass_utils import run_bass_kernel_spmd

F32 = mybir.dt.float32
BF16 = mybir.dt.bfloat16
U32 = mybir.dt.uint32
ALU = mybir.AluOpType
AF = mybir.ActivationFunctionType
AX = mybir.AxisListType

NCORE = 8
D = 1024
ALPHA = 4.0 ** 0.25
EPS = 1e-6
NDS = 24


class Buf:
    __slots__ = ("ap", "w", "r", "name", "excl")

    def __init__(self, ap, name="", excl=False):
        self.ap = ap
        self.w = None
        self.r = {}
        self.name = name
        self.excl = excl


class Eng:
    def __init__(self, name):
        self.name = name
        self.ops = []
        self.clock = {}
        self.semid = None
        self.dma_sems = []
        self.dma_i = 0


class KB:
    def __init__(self):
        self.nc = bass.Bass("TRN2", target_bir_lowering=False)
        nc = self.nc
        self.E = {n: Eng(n) for n in ("pe", "act", "dve", "pool", "sp")}
        self.sems = []
        self.semcount = []
        for n in ("pe", "act", "dve", "pool", "sp"):
            self.E[n].semid = self._newsem("e_" + n)
        for n in ("sp", "act", "pool"):
            self.E[n].dma_sems = [self._newsem(f"d_{n}{i}") for i in range(NDS)]
        self.hist = {}
        self.arena = nc.alloc_sbuf_tensor("arena", [128, ARENA_W], F32).ap()
        self.top = 0
        self.psb = [Buf(nc.alloc_psum_tensor(f"ps{i}", [128, 512], F32).ap(), f"ps{i}", excl=True) for i in range(8)]
        self.outs_ev = []
        self.ndram = 0

    def _newsem(self, name):
        self.sems.append(self.nc.alloc_semaphore(name))
        self.semcount.append(0)
        return len(self.sems) - 1

    def T(self, shape, dtype=F32, name=""):
        per = int(np.prod(shape[1:]))
        words = per if dtype in (F32, U32, mybir.dt.int32) else (per + 1) // 2
        words = (words + 7) // 8 * 8
        assert self.top + words <= ARENA_W, (name, self.top, words)
        ap = self.arena[:, self.top:self.top + words]
        if not hasattr(self, "reg"):
            self.reg = {}
        self.reg[name + "@" + str(self.top)] = (self.top, list(shape), str(dtype))
        self.top += words
        if dtype != F32:
            ap = ap.bitcast(dtype)
        ap = ap[:, 0:per]
        if shape[0] != 128:
            ap = ap[0:shape[0], :]
        if len(shape) == 3:
            ap = ap.rearrange("p (a b) -> p a b", b=shape[2])
        elif len(shape) == 4:
            ap = ap.rearrange("p (a b c) -> p a b c", b=shape[2], c=shape[3])
        return Buf(ap, name)

    def dram(self, shape, dtype=F32, name=None, kind="Internal"):
        if name is None:
            self.ndram += 1
            name = f"scr{self.ndram}"
        return Buf(self.nc.dram_tensor(name, list(shape), dtype, kind=kind).ap(), name)

    def _learn(self, eng, s, c):
        h = self.hist.get((s, c))
        if h:
            ck = eng.clock
            for k, v in h.items():
                if ck.get(k, 0) < v:
                    ck[k] = v
        if eng.clock.get(s, 0) < c:
            eng.clock[s] = c

    capture = None

    def interleave(self, thunks):
        lists = []
        for th in thunks:
            self.capture = []
            th()
            lists.append(self.capture)
            self.capture = None
        n = max((len(l) for l in lists), default=0)
        for i in range(n):
            for l in lists:
                if i < len(l):
                    self.emit(*l[i])

    def emit(self, en, fn, R, W, dma=False, cc=False):
        if self.capture is not None:
            self.capture.append((en, fn, list(R), list(W), dma, cc))
            return None
        eng = self.E[en]
        deps = {}
        for b in R:
            if b.w is not None and deps.get(b.w[0], 0) < b.w[1]:
                deps[b.w[0]] = b.w[1]
            if b.excl:
                for s, c in b.r.items():
                    if s != eng.semid and deps.get(s, 0) < c:
                        deps[s] = c
        for b in W:
            if b.w is not None and deps.get(b.w[0], 0) < b.w[1]:
                deps[b.w[0]] = b.w[1]
            for s, c in b.r.items():
                if deps.get(s, 0) < c:
                    deps[s] = c
        waits = []
        for s, c in deps.items():
            if en == "pe" and s == eng.semid:
                continue
            if eng.clock.get(s, 0) >= c:
                continue
            waits.append((s, c))
            self._learn(eng, s, c)
        if dma or cc:
            slot = eng.dma_i % NDS
            eng.dma_i += 1
            s = eng.dma_sems[slot]
            prev = self.semcount[s]
            if prev and eng.clock.get(s, 0) < prev:
                waits.append((s, prev))
                self._learn(eng, s, prev)
            inc = 1 if cc else 16
            self.semcount[s] = prev + inc
            ev = (s, prev + inc)
        else:
            s = eng.semid
            self.semcount[s] += 1
            ev = (s, self.semcount[s])
            inc = 1
        self.hist[ev] = dict(eng.clock)
        for b in R:
            if b.r.get(ev[0], 0) < ev[1]:
                b.r[ev[0]] = ev[1]
        for b in W:
            b.w = ev
            b.r = {}
        eng.ops.append((waits, fn, s, inc, cc))
        return ev

    def barrier(self):
        last = {s: c for s, c in enumerate(self.semcount) if c}
        for en, eng in self.E.items():
            waits = []
            for s, c in last.items():
                if eng.clock.get(s, 0) < c:
                    waits.append((s, c))
                    eng.clock[s] = c
            if waits:
                eng.ops.append((waits, None, None, 0, False))

    def mm(self, out, lhsT, rhs, start, stop, R, W):
        return self.emit("pe", lambda e: e.matmul(out, lhsT, rhs, start=start, stop=stop), R, W)

    def tr(self, out, in_, ident, R, W):
        return self.emit("pe", lambda e: e.transpose(out, in_, ident), R, W)

    def V(self, fn, R, W):
        return self.emit("dve", fn, R, W)

    def A(self, fn, R, W):
        return self.emit("act", fn, R, W)

    def G(self, fn, R, W):
        return self.emit("pool", fn, R, W)

    def dma(self, out, in_, R, W, q="sp", is_out=False):
        ev = self.emit(q, lambda e: e.dma_start(out=out, in_=in_), R, W, dma=True)
        if is_out:
            self.outs_ev.append(ev)
        return ev

    def coll(self, kind, op, src, dst):
        grp = [list(range(NCORE))]
        return self.emit("pool", lambda e: e.collective_compute(kind, op, replica_groups=grp,
                                                                  ins=[src.ap.opt()], outs=[dst.ap.opt()]),
                         [src], [dst], cc=True)

    def finish(self):
        eng = self.E["sp"]
        waits = []
        for (s, c) in self.outs_ev:
            if eng.clock.get(s, 0) < c:
                waits.append((s, c))
                eng.clock[s] = c
        eng.ops.append((waits, None, None, 0, False))
        self.barrier()
        nc = self.nc
        sems = self.sems
        with nc.Block() as block:
            def run(eng):
                def body(e):
                    if eng.name == "pool":
                        self.bc_reg = e.to_reg(20479)
                    for (waits, fn, s, inc, cc) in eng.ops:
                        for (ws, wc) in waits:
                            e.wait_ge(sems[ws], wc)
                        if fn is not None:
                            ins = fn(e)
                            if cc:
                                ins.then_inc(sems[s])
                            else:
                                ins.then_inc(sems[s], inc)
                return body
            block.tensor(run(self.E["pe"]))
            block.scalar(run(self.E["act"]))
            block.vector(run(self.E["dve"]))
            block.gpsimd(run(self.E["pool"]))
            block.sync(run(self.E["sp"]))
        return nc


ARENA_W = 45056


NT = 80
NTP = 64
I32 = mybir.dt.int32
import os
KSTOP = os.environ.get("KSTOP", "")
KCUT = int(os.environ.get("KCUT", "0"))
KSKIP = os.environ.get("KSKIP", "").split(",")


LASTK = [None]


def build():
    k = KB()
    LASTK[0] = k
    nc = k.nc
    PS = k.psb

    def din(name, shape, dtype=F32):
        return k.dram(shape, dtype, name=name, kind="ExternalInput")

    din_late = din

    def dout(name, shape, dtype=F32):
        return k.dram(shape, dtype, name=name, kind="ExternalOutput")

    xin = din("xin", [NT * 128, D])
    cvT = din("cvT", [128, 8, 3]); w_mod = din("w_mod", [2, 1024, 6144]); b_mod = din("b_mod", [2, 6144])
    lng = din("lng", [4, D]); lnb = din("lnb", [4, D])
    w_in_hd = din("w_in_hd", [1024, 3072]); w_g1 = din("w_g1", [1024, 32])
    w_gf2a = din("w_gf2a", [17, 512]); w_gb2a = din("w_gb2a", [17, 512])
    gnorm = din("gnorm", [1, 1024]); w_out = din("w_out", [1024, 1024])
    S0f = din("S0f", [2, 4, 128, 256]); S0b = din("S0b", [2, 4, 128, 256])
    cmask = din("cmask", [64, 6, 64]); identin = din("identin", [128, 128])
    w_router = din("w_router", [2, 1024, 16])
    wg = din("wg", [2, 16, 1024, 2048]); wu = din("wu", [2, 16, 1024, 2048]); wd = din("wd", [2, 16, 2048, 1024])
    iota_in = din("iota_in", [128, 2])
    wa_in = din("wa_in", [1024, 1536]); wa_out = din("wa_out", [1024, 1024]); gqk = din("gqk", [2, 128])
    ck = din("ck", [2, 2, 256, 128]); cv_ = din("cv_", [2, 2, 256, 128]); rope = din("rope", [1024, 2, 64])
    yout = dout("yout", [NT * 128, D])
    nf = dout("nf", [32, 4, 128, 256]); nb = dout("nb", [32, 4, 128, 256])
    nk = dout("nk", [32, 2, 256, 128]); nv = dout("nv", [32, 2, 256, 128])
    MOD = k.dram([3, 12288], F32)
    X1 = k.dram([NT * 128, D], F32)
    X2 = k.dram([NT * 128, D], F32)
    H2 = k.dram([NT * 128, D], BF16)
    XE = k.dram([20480, D], BF16)
    YE = k.dram([20480, D], F32)

    identb = k.T([128, 128], BF16, "identb")
    identf = k.T([128, 128], F32, "identf")
    cm = k.T([64, 6, 64], F32, "cm")
    negcol = k.T([64, 1], F32, "negcol")
    lnst = k.T([128, 2, 6], F32); lnmv = k.T([128, 2], F32)
    affA = k.T([128, NT, 16], F32, "affA")
    k.dma(identf.ap, identin.ap, [identin], [identf])
    k.V(lambda e: e.tensor_copy(out=identb.ap, in_=identf.ap), [identf], [identb])
    k.dma(cm.ap, cmask.ap, [cmask], [cm])
    k.V(lambda e: e.memset(negcol.ap, -1.0 / 16.0), [], [negcol])
    PERSIST_TOP = k.top

    psi = {}
    pspool = [list(range(8))]

    def nps():
        pool = pspool[0]
        key = tuple(pool)
        psi[key] = (psi.get(key, -1) + 1) % len(pool)
        return PS[pool[psi[key]]]

    def bcast_row(dst, src_dram, row, c0=0):
        n = dst.ap.shape[1]
        k.dma(dst.ap, src_dram.ap[row:row + 1, c0:c0 + n].to_broadcast([128, n]), [src_dram], [dst])

    def get_mod(dst, layer, j, row, plus1=False):
        bcast_row(dst, MOD, row, layer * 6144 + j * 1024)
        if plus1:
            k.V(lambda e: e.tensor_scalar_add(out=dst.ap, in0=dst.ap, scalar1=1.0), [dst], [dst])

    def layernorm(xt, gt, bt):
        st = lnst; mv = lnmv
        k.V(lambda e: e.bn_stats(out=st.ap[:, 0, :], in_=xt.ap[:, 0:512]), [xt], [st])
        k.V(lambda e: e.bn_stats(out=st.ap[:, 1, :], in_=xt.ap[:, 512:1024]), [xt, st], [st])
        k.V(lambda e: e.bn_aggr(out=mv.ap, in_=st.ap.rearrange("p a b -> p (a b)")), [st], [mv])
        k.V(lambda e: e.tensor_scalar_add(out=mv.ap[:, 1:2], in0=mv.ap[:, 1:2], scalar1=EPS), [mv], [mv])
        k.A(lambda e: e.activation(out=mv.ap[:, 1:2], in_=mv.ap[:, 1:2], func=AF.Sqrt), [mv], [mv])
        k.V(lambda e: e.reciprocal(out=mv.ap[:, 1:2], in_=mv.ap[:, 1:2]), [mv], [mv])
        k.V(lambda e: e.tensor_scalar(out=xt.ap, in0=xt.ap, scalar1=mv.ap[:, 0:1], scalar2=mv.ap[:, 1:2],
                                      op0=ALU.subtract, op1=ALU.mult), [xt, mv], [xt])
        k.V(lambda e: e.tensor_mul(out=xt.ap, in0=xt.ap, in1=gt.ap), [xt, gt], [xt])
        k.V(lambda e: e.tensor_add(out=xt.ap, in0=xt.ap, in1=bt.ap), [xt, bt], [xt])

    def load_w_bf16(dst, src_ap, src_buf, kchunks, cols, stage, dcol=0):
        i = 0
        cw = min(cols, 1024)
        step = max(1, 1024 // cw)
        for c0 in range(0, cols, cw):
            cwi = min(cw, cols - c0)
            for k0 in range(0, kchunks, step):
                k1 = min(kchunks, k0 + step)
                st = stage[i % len(stage)]
                i += 1
                v = st.ap[:, 0:(k1 - k0) * cwi].rearrange("p (a b) -> p a b", b=cwi)
                k.dma(v, src_ap[k0 * 128:k1 * 128, c0:c0 + cwi].rearrange("(a p) c -> p a c", p=128), [src_buf], [st])
                k.G(lambda e, v=v, k0=k0, k1=k1, c0=c0, cwi=cwi: e.tensor_copy(out=dst.ap[:, k0:k1, dcol + c0:dcol + c0 + cwi], in_=v), [st], [dst])

    cvt = k.T([128, 8, 3], F32)
    wst = [k.T([128, 8, 512], F32, f"wst{i}") for i in range(2)]
    bm = k.T([3, 512], F32); mo = k.T([3, 512], F32)
    k.dma(cvt.ap, cvT.ap, [cvT], [cvt])
    k.A(lambda e: e.activation(out=cvt.ap, in_=cvt.ap, func=AF.Silu), [cvt], [cvt])
    for i in range(2):
        for cb in range(12):
            w_ = wst[(i * 12 + cb) % 2]
            k.dma(w_.ap, w_mod.ap[i, :, cb * 512:(cb + 1) * 512].rearrange("(k p) c -> p k c", p=128), [w_mod], [w_])
            k.dma(bm.ap, b_mod.ap[i:i + 1, cb * 512:(cb + 1) * 512].to_broadcast([3, 512]), [b_mod], [bm])
            p = nps()
            for kk in range(8):
                k.mm(p.ap[0:3, :], cvt.ap[:, kk, :], w_.ap[:, kk, :], kk == 0, kk == 7, [cvt, w_], [p])
            k.V(lambda e, p=p: e.tensor_add(out=mo.ap, in0=p.ap[0:3, :], in1=bm.ap), [p, bm], [mo])
            k.dma(MOD.ap[:, i * 6144 + cb * 512:i * 6144 + (cb + 1) * 512], mo.ap, [mo], [MOD])
    k.barrier()
    k.top = PERSIST_TOP
    if KSTOP == "M":
        xt = k.T([128, D], F32)
        for j in range(12):
            k.dma(xt.ap, MOD.ap[0:1, j * 1024:(j + 1) * 1024].to_broadcast([128, 1024]), [MOD], [xt])
            k.dma(yout.ap[j * 128:(j + 1) * 128, :], xt.ap, [xt], [yout], is_out=True)
        return k.finish()

    def row_of_tile(t):
        return 0 if t < NTP else (1 if t < NTP + 8 else 2)

    def post_mixer(t, xt, yps, gaT, lgT, lbT, sc2T, sh2T, wrT, tmpT, h2bT, h2TT):
        for half in range(2):
            k.V(lambda e, half=half: e.tensor_mul(out=tmpT.ap[:, half * 512:(half + 1) * 512], in0=yps[half].ap,
                                                 in1=gaT.ap[:, half * 512:(half + 1) * 512]), [yps[half], gaT], [tmpT])
        k.V(lambda e: e.scalar_tensor_tensor(out=xt.ap, in0=xt.ap, scalar=ALPHA, in1=tmpT.ap, op0=ALU.mult, op1=ALU.add), [xt, tmpT], [xt])
        layernorm(xt, lgT, lbT)
        k.dma(X1.ap[t * 128:(t + 1) * 128, :], xt.ap, [xt], [X1])
        k.V(lambda e: e.tensor_mul(out=tmpT.ap, in0=xt.ap, in1=sc2T.ap), [xt, sc2T], [tmpT])
        k.V(lambda e: e.tensor_add(out=tmpT.ap, in0=tmpT.ap, in1=sh2T.ap), [tmpT, sh2T], [tmpT])
        k.A(lambda e: e.copy(out=h2bT.ap, in_=tmpT.ap), [tmpT], [h2bT])
        k.dma(H2.ap[t * 128:(t + 1) * 128, :], h2bT.ap, [h2bT], [H2])
        for half in range(2):
            p = nps()
            for kk in range(4):
                kc = half * 4 + kk
                k.tr(p.ap[:, kk * 128:(kk + 1) * 128], tmpT.ap[:, kc * 128:(kc + 1) * 128], identf.ap, [tmpT, identf], [p])
            k.A(lambda e, p=p, half=half: e.copy(out=h2TT.ap[:, half * 4:half * 4 + 4, :], in_=p.ap.rearrange("p (a b) -> p a b", b=128)), [p], [h2TT])
        pl = nps()
        for kk in range(8):
            k.mm(pl.ap[:, 0:16], h2TT.ap[:, kk, :], wrT.ap[:, kk, :], kk == 0, kk == 7, [h2TT, wrT], [pl])
        mx = lnmv
        k.V(lambda e, pl=pl: e.tensor_reduce(out=mx.ap[:, 0:1], in_=pl.ap[:, 0:16], axis=AX.X, op=ALU.max), [pl], [mx])
        k.V(lambda e, pl=pl: e.tensor_scalar(out=affA.ap[:, t, :], in0=pl.ap[:, 0:16], scalar1=mx.ap[:, 0:1], scalar2=None, op0=ALU.subtract), [pl, mx], [affA])
        k.A(lambda e: e.activation(out=affA.ap[:, t, :], in_=affA.ap[:, t, :], func=AF.Exp), [affA], [affA])
        k.V(lambda e: e.tensor_reduce(out=mx.ap[:, 1:2], in_=affA.ap[:, t, :], axis=AX.X, op=ALU.add), [affA], [mx])
        k.V(lambda e: e.reciprocal(out=mx.ap[:, 1:2], in_=mx.ap[:, 1:2]), [mx], [mx])
        k.V(lambda e: e.tensor_scalar(out=affA.ap[:, t, :], in0=affA.ap[:, t, :], scalar1=mx.ap[:, 1:2], scalar2=None, op0=ALU.mult), [affA, mx], [affA])

    def gla_layer():
        nonlocal_top = PERSIST_TOP
        k.top = PERSIST_TOP
        stage = [k.T([128, 1024], F32, f"stg{i}") for i in range(2)]
        Wo = k.T([128, 8, 1024], BF16, "Wo")
        Wg1 = k.T([128, 8, 32], BF16, "Wg1")
        W2f = k.T([17, 512], BF16); W2b = k.T([17, 512], BF16)
        gnt = k.T([128, 1024], F32)
        scT = k.T([128, D]); shT = k.T([128, D]); tmpT = k.T([128, D])
        wrT = k.T([128, 8, 16], F32)
        h2bT = k.T([128, D], BF16); h2TT = k.T([128, 8, 128], F32)
        xts = [k.T([128, D], F32, "xt0")] * 2
        g1aug = [k.T([17, 64], BF16, f"g1aug{i}") for i in range(2)]
        kgT = k.T([128, 64], BF16, "kgT")
        hb = k.T([128, D], BF16, "hb")
        load_w_bf16(Wo, w_out.ap, w_out, 8, 1024, stage)
        load_w_bf16(Wg1, w_g1.ap, w_g1, 8, 32, stage)
        for dst, src in ((W2f, w_gf2a), (W2b, w_gb2a)):
            k.dma(stage[0].ap[0:17, 0:512], src.ap, [src], [stage[0]])
            k.V(lambda e, dst=dst: e.tensor_copy(out=dst.ap, in_=stage[0].ap[0:17, 0:512]), [stage[0]], [dst])
        bcast_row(gnt, gnorm, 0)
        k.dma(wrT.ap, w_router.ap[0].rearrange("(k p) c -> p k c", p=128), [w_router], [wrT])
        for t_ in g1aug:
            k.V(lambda e, t_=t_: e.memset(t_.ap, 1.0), [], [t_])
        BASE_TOP = k.top

        def alloc_post(row):
            P = dict(ga=k.T([128, D]), sc2=k.T([128, D]), sh2=k.T([128, D]), lg=k.T([128, D]), lb=k.T([128, D]))
            get_mod(P["ga"], 0, 2, row); get_mod(P["sc2"], 0, 4, row, plus1=True); get_mod(P["sh2"], 0, 3, row)
            bcast_row(P["lg"], lng, 0); bcast_row(P["lb"], lnb, 0)
            return P

        def run_seqs(seqs, Wbuf, per_head_load, P):
            T_max = max(nt for _, nt, _, _, _ in seqs) * 128
            hT = k.T([128, 8, T_max], BF16, "hT")
            ogT = k.T([128, 8, T_max], BF16, "ogT")
            SCR_TOP = k.top
            scr = gla_scratch(T_max)
            SEQ_TOP = k.top
            cur_row = [None]
            for (t0, ntile, row, S0, so) in seqs:
                T = ntile * 128
                if cur_row[0] != row:
                    cur_row[0] = row
                    get_mod(scT, 0, 1, row, plus1=True); get_mod(shT, 0, 0, row)
                for ti in range(ntile):
                    xt = xts[ti % 2]
                    k.dma(xt.ap, xin.ap[(t0 + ti) * 128:(t0 + ti + 1) * 128, :], [xin], [xt])
                    k.V(lambda e, xt=xt: e.tensor_mul(out=tmpT.ap, in0=xt.ap, in1=scT.ap), [xt, scT], [tmpT])
                    k.V(lambda e: e.tensor_add(out=hb.ap, in0=tmpT.ap, in1=shT.ap), [tmpT, shT], [hb])
                    for half in range(2):
                        if "hT" in KSKIP:
                            continue
                        p = nps(); pv = p.ap.bitcast(BF16)
                        for kk in range(4):
                            kc = half * 4 + kk
                            k.tr(pv[:, kk * 128:(kk + 1) * 128], hb.ap[:, kc * 128:(kc + 1) * 128], identb.ap, [hb, identb], [p])
                        k.A(lambda e, pv=pv, half=half, ti=ti: e.copy(out=hT.ap[:, half * 4:half * 4 + 4, ti * 128:(ti + 1) * 128],
                                                                      in_=pv[:, 0:512].rearrange("p (a b) -> p a b", b=128)), [p], [hT])
                for h in range(4):
                    if "heads" in KSKIP:
                        continue
                    if per_head_load:
                        load_w_bf16(Wbuf, w_in_hd.ap[:, h * 768:(h + 1) * 768], w_in_hd, 8, 768, stage)
                        wb = 0
                    else:
                        wb = h * 768

                    def cb(c, oh, h=h):
                        p = nps(); pv = p.ap.bitcast(BF16)
                        for kk in range(2):
                            k.tr(pv[:, kk * 64:(kk + 1) * 64], oh.ap[:, kk * 128:(kk + 1) * 128], identb.ap[0:64, 0:64], [oh, identb], [p])
                        k.A(lambda e, pv=pv: e.copy(out=ogT.ap[:, 2 * h:2 * h + 2, c * 64:(c + 1) * 64],
                                                    in_=pv[:, 0:128].rearrange("p (a b) -> p a b", b=64)), [p], [ogT])
                    S0h = None if S0 is None else (S0[0].ap[S0[2], h], S0[0], S0[1].ap[S0[2], h], S0[1])
                    Sf, Sb = gla_head(scr, hT, T, Wbuf, wb, W2f, W2b, h * 128, gnt.ap[0:64, h * 256:(h + 1) * 256], gnt, S0h, cb,
                                      Wg1, g1aug, kgT)
                    if so is not None:
                        k.dma(nf.ap[so, h], Sf.ap, [Sf], [nf], is_out=True)
                        k.dma(nb.ap[so, h], Sb.ap, [Sb], [nb], is_out=True)
                if P is None:
                    k.barrier()
                    k.top = SCR_TOP
                    PP = alloc_post(row)
                else:
                    PP = P
                for ti in range(ntile):
                    if "post" in KSKIP:
                        continue
                    xt = xts[ti % 2]
                    k.dma(xt.ap, xin.ap[(t0 + ti) * 128:(t0 + ti + 1) * 128, :], [xin], [xt])
                    yps = []
                    for half in range(2):
                        py = nps()
                        for kk in range(8):
                            k.mm(py.ap, ogT.ap[:, kk, ti * 128:(ti + 1) * 128], Wo.ap[:, kk, half * 512:(half + 1) * 512], kk == 0, kk == 7, [ogT, Wo], [py])
                        yps.append(py)
                    post_mixer(t0 + ti, xt, yps, PP["ga"], PP["lg"], PP["lb"], PP["sc2"], PP["sh2"], wrT, tmpT, h2bT, h2TT)
                if P is None:
                    k.barrier()

        k.top = BASE_TOP
        Wp4 = k.T([128, 8, 3072], BF16, "Wp4")
        load_w_bf16(Wp4, w_in_hd.ap, w_in_hd, 8, 3072, stage)
        nseq = 32 if KSTOP not in ("G1", "G0") else 1
        run_seqs([(2 * s, 2, 0, None, s) for s in range(nseq)], Wp4, False, None)
        k.barrier()
        k.top = BASE_TOP
        Whd = k.T([128, 8, 768], BF16, "Whd")
        if KSTOP != "G0":
            run_seqs([(NTP + 8 * b, 8, 1 + b, (S0f, S0b, b), None) for b in range(2)], Whd, True, None)
        k.barrier()
        k.top = PERSIST_TOP

    def gla_scratch(T):
        class S_: pass
        S = S_()
        nch = T // 64
        S.vS = k.T([64, nch, 256], BF16, "vS")
        S.srS = k.T([64, nch, 256], F32, "srS")
        S.qgT = k.T([128, nch, 2, 64], BF16, "qgT")
        S.AT = k.T([64, nch, 2, 64], BF16, "AT")
        S.Sin = k.T([128, nch, 2, 256], BF16, "Sin")
        S.kdb = k.T([64, nch, 128], BF16, "kdb")
        S.decb = k.T([128, nch], F32, "decb")
        S.Sf = k.T([128, 256], F32, "Sf"); S.Sb = k.T([128, 256], F32, "Sb")
        S.decf = [k.T([128, 1], F32, f"decf{i}") for i in range(2)]
        S.qs = [k.T([64, 128], F32, f"qs{i}") for i in range(2)]; S.ks = [k.T([64, 128], F32, f"ks{i}") for i in range(2)]
        S.lf = [k.T([64, 2, 128], F32, f"lf{i}") for i in range(2)]
        S.ex = [k.T([64, 3, 128], F32, f"ex{i}") for i in range(2)]
        S.qg = [k.T([64, 128], BF16, f"qg{i}") for i in range(2)]; S.kg = [k.T([64, 128], BF16, f"kg{i}") for i in range(2)]; S.kd = k.T([64, 128], BF16, "kd")
        S.sq = [k.T([64, 256], F32, f"sq{i}") for i in range(2)] if T <= 256 else [k.T([64, 256], F32, "sq0")] * 2; S.of = [k.T([64, 256], F32, f"of{i}") for i in range(2)] if T <= 256 else [k.T([64, 256], F32, "of0")] * 2; S.ss = [k.T([64, 1], F32, f"ss{i}") for i in range(2)]
        S.kgT = [k.T([128, 64], BF16, f"kgT{i}") for i in range(2)]
        S.ogh = [k.T([64, 256], BF16, f"ogh{i}") for i in range(2)]
        return S

    def gla_head(scr, hT, T, Wb, wb, W2f, W2b, w2c, gn_ap, gn_buf, S0, out_cb, Wg1, g1aug, kgT):
        nch = T // 64
        vS = scr.vS
        srS = scr.srS
        qgT = scr.qgT
        AT = scr.AT
        Sin = scr.Sin
        kdb = scr.kdb
        decb = scr.decb
        Sf = scr.Sf
        Sb = scr.Sb
        kd = scr.kd
        ogh = scr.ogh
        if S0 is None:
            k.V(lambda e: e.memset(Sf.ap, 0.0), [], [Sf])
            k.V(lambda e: e.memset(Sb.ap, 0.0), [], [Sb])
        else:
            k.dma(Sf.ap, S0[0], [S0[1]], [Sf])
            k.dma(Sb.ap, S0[2], [S0[3]], [Sb])
        def p1a(c):
                if KCUT == -1:
                    return
                qs = scr.qs[c % 2]; ks = scr.ks[c % 2]; lf = scr.lf[c % 2]; decf = scr.decf[c % 2]
                tk = slice(c * 64, (c + 1) * 64)
                pq = nps()
                for kk in range(8):
                    k.mm(pq.ap[0:64, 0:256], hT.ap[:, kk, tk], Wb.ap[:, kk, wb:wb + 256], kk == 0, kk == 7, [hT, Wb], [pq])
                k.A(lambda e, pq=pq: e.copy(out=qs.ap, in_=pq.ap[0:64, 0:128]), [pq], [qs])
                k.A(lambda e, pq=pq: e.copy(out=ks.ap, in_=pq.ap[0:64, 128:256]), [pq], [ks])
                if KCUT == -2:
                    return
                pv_ = nps()
                for kk in range(8):
                    k.mm(pv_.ap[0:64, 0:512], hT.ap[:, kk, tk], Wb.ap[:, kk, wb + 256:wb + 768], kk == 0, kk == 7, [hT, Wb], [pv_])
                k.V(lambda e, pv_=pv_, c=c: e.tensor_copy(out=vS.ap[:, c, :], in_=pv_.ap[0:64, 0:256]), [pv_], [vS])
                k.A(lambda e, pv_=pv_, c=c: e.activation(out=srS.ap[:, c, :], in_=pv_.ap[0:64, 256:512], func=AF.Exp, scale=-1.0), [pv_], [srS])
                k.V(lambda e, c=c: e.tensor_scalar_add(out=srS.ap[:, c, :], in0=srS.ap[:, c, :], scalar1=1.0), [srS], [srS])
                k.V(lambda e, c=c: e.reciprocal(out=srS.ap[:, c, :], in_=srS.ap[:, c, :]), [srS], [srS])
                k.V(lambda e, pv_=pv_, c=c: e.tensor_mul(out=srS.ap[:, c, :], in0=srS.ap[:, c, :], in1=pv_.ap[0:64, 256:512]), [srS, pv_], [srS])
                if KCUT == 1:
                    return
                for d_, (W2_, ga) in enumerate(((W2f, g1aug[0]), (W2b, g1aug[1]))):
                    pg = nps()
                    for kk in range(8):
                        k.mm(pg.ap[0:16, 0:64], Wg1.ap[:, kk, d_ * 16:(d_ + 1) * 16], hT.ap[:, kk, tk], kk == 0, kk == 7, [Wg1, hT], [pg])
                    k.V(lambda e, pg=pg, ga=ga: e.tensor_copy(out=ga.ap[0:16, :], in_=pg.ap[0:16, 0:64]), [pg], [ga])
                    pz = nps()
                    k.mm(pz.ap[0:64, 0:128], ga.ap, W2_.ap[:, w2c:w2c + 128], True, True, [ga, W2_], [pz])
                    k.A(lambda e, pz=pz, d_=d_: e.activation(out=lf.ap[:, d_, :], in_=pz.ap[0:64, 0:128], func=AF.Exp, scale=-1.0), [pz], [lf])
                    k.V(lambda e, d_=d_: e.tensor_scalar_add(out=lf.ap[:, d_, :], in0=lf.ap[:, d_, :], scalar1=1.0), [lf], [lf])
                    k.A(lambda e, d_=d_: e.activation(out=lf.ap[:, d_, :], in_=lf.ap[:, d_, :], func=AF.Ln), [lf], [lf])
                if KCUT == 2:
                    return
        def p1d(c, d_):
                qs = scr.qs[c % 2]; ks = scr.ks[c % 2]; lf = scr.lf[c % 2]; decf = scr.decf[c % 2]
                ex = scr.ex[d_]; qg = scr.qg[d_]; kg = scr.kg[d_]; kgT = scr.kgT[d_]
                pG = nps()
                k.mm(pG.ap[0:64, 0:128], cm.ap[:, 2 * d_, :], lf.ap[:, d_, :], True, True, [cm, lf], [pG])
                k.mm(pG.ap[0:64, 128:256], cm.ap[:, 2 * d_ + 1, :], lf.ap[:, d_, :], True, True, [cm, lf], [pG])
                pdc = nps()
                k.mm(pdc.ap[:, 0:1], lf.ap[:, d_, :], negcol.ap, True, True, [lf, negcol], [pdc])
                if d_ == 0:
                    k.A(lambda e, pdc=pdc: e.activation(out=decf.ap, in_=pdc.ap[:, 0:1], func=AF.Exp), [pdc], [decf])
                else:
                    k.A(lambda e, pdc=pdc, c=c: e.activation(out=decb.ap[:, c:c + 1], in_=pdc.ap[:, 0:1], func=AF.Exp), [pdc], [decb])
                k.A(lambda e, pG=pG: e.activation(out=ex.ap[:, 0, :], in_=pG.ap[0:64, 0:128], func=AF.Exp), [pG], [ex])
                k.A(lambda e, pG=pG: e.activation(out=ex.ap[:, 1, :], in_=pG.ap[0:64, 0:128], func=AF.Exp, scale=-1.0), [pG], [ex])
                k.A(lambda e, pG=pG: e.activation(out=ex.ap[:, 2, :], in_=pG.ap[0:64, 128:256], func=AF.Exp), [pG], [ex])
                if KCUT == 3:
                    return
                k.V(lambda e: e.scalar_tensor_tensor(out=qg.ap, in0=qs.ap, scalar=128.0 ** -0.5, in1=ex.ap[:, 0, :],
                                                     op0=ALU.mult, op1=ALU.mult), [qs, ex], [qg])
                k.V(lambda e: e.tensor_mul(out=kg.ap, in0=ks.ap, in1=ex.ap[:, 1, :]), [ks, ex], [kg])
                kdd = kd if d_ == 0 else kdb
                kd_ap = kd.ap if d_ == 0 else kdb.ap[:, c, :]
                k.V(lambda e, kd_ap=kd_ap: e.tensor_mul(out=kd_ap, in0=ks.ap, in1=ex.ap[:, 2, :]), [ks, ex], [kdd])
                pT = nps(); pTv = pT.ap.bitcast(BF16)
                k.tr(pTv[:, 0:64], qg.ap, identb.ap[0:64, 0:64], [qg, identb], [pT])
                k.tr(pTv[:, 64:128], kg.ap, identb.ap[0:64, 0:64], [kg, identb], [pT])
                k.A(lambda e, pTv=pTv, c=c, d_=d_: e.copy(out=qgT.ap[:, c, d_, :], in_=pTv[:, 0:64]), [pT], [qgT])
                k.V(lambda e, pTv=pTv: e.tensor_copy(out=kgT.ap, in_=pTv[:, 64:128]), [pT], [kgT])
                if KCUT == 4:
                    return
                pA = nps()
                k.mm(pA.ap[0:64, 0:64], kgT.ap, qgT.ap[:, c, d_, :], True, True, [kgT, qgT], [pA])
                k.V(lambda e, pA=pA, c=c, d_=d_: e.tensor_mul(out=AT.ap[:, c, d_, :], in0=pA.ap[0:64, 0:64], in1=cm.ap[:, 4 + d_, :]), [pA, cm], [AT])
                if d_ == 0:
                    pS = nps()
                    k.mm(pS.ap[:, 0:256], kd.ap, vS.ap[:, c, :], True, True, [kd, vS], [pS])
                    k.A(lambda e, c=c: e.copy(out=Sin.ap[:, c, 0, :], in_=Sf.ap), [Sf], [Sin])
                    k.V(lambda e, pS=pS: e.scalar_tensor_tensor(out=Sf.ap, in0=Sf.ap, scalar=decf.ap[:, 0:1], in1=pS.ap[:, 0:256],
                                                                 op0=ALU.mult, op1=ALU.add), [Sf, decf, pS], [Sf])

        def with_pool(pool, fn, *args):
            def th():
                old = pspool[0]
                pspool[0] = pool
                try:
                    fn(*args)
                finally:
                    pspool[0] = old
            return th
        if KCUT != 0:
            for c in range(nch):
                p1a(c)
                for d_ in range(2):
                    p1d(c, d_)
        else:
            k.interleave([with_pool([6, 7], p1a, 0)])
            for c in range(nch):
                ths = [with_pool([0, 1, 2], p1d, c, 0), with_pool([3, 4, 5], p1d, c, 1)]
                if c + 1 < nch:
                    ths.append(with_pool([6, 7], p1a, c + 1))
                k.interleave(ths)

        if KCUT != 0 and KCUT <= 5:
            return Sf, Sb
        for c in range(nch - 1, -1, -1):
            pS = nps()
            k.mm(pS.ap[:, 0:256], kdb.ap[:, c, :], vS.ap[:, c, :], True, True, [kdb, vS], [pS])
            k.A(lambda e, c=c: e.copy(out=Sin.ap[:, c, 1, :], in_=Sb.ap), [Sb], [Sin])
            k.V(lambda e, pS=pS, c=c: e.scalar_tensor_tensor(out=Sb.ap, in0=Sb.ap, scalar=decb.ap[:, c:c + 1], in1=pS.ap[:, 0:256],
                                                              op0=ALU.mult, op1=ALU.add), [Sb, decb, pS], [Sb])
        if KCUT == 6:
            return Sf, Sb
        def p2(c):
                sq = scr.sq[c % 2]; of = scr.of[c % 2]; ss = scr.ss[c % 2]
                po = nps()
                o_ap = po.ap[0:64, 0:256]
                k.mm(o_ap, AT.ap[:, c, 0, :], vS.ap[:, c, :], True, False, [AT, vS], [po])
                k.mm(o_ap, AT.ap[:, c, 1, :], vS.ap[:, c, :], False, False, [AT, vS], [po])
                k.mm(o_ap, qgT.ap[:, c, 0, :], Sin.ap[:, c, 0, :], False, False, [qgT, Sin], [po])
                k.mm(o_ap, qgT.ap[:, c, 1, :], Sin.ap[:, c, 1, :], False, True, [qgT, Sin], [po])
                k.A(lambda e, po=po: e.copy(out=of.ap, in_=po.ap[0:64, 0:256]), [po], [of])
                if KCUT == 7:
                    return
                k.V(lambda e: e.tensor_mul(out=sq.ap, in0=of.ap, in1=of.ap), [of], [sq])
                k.V(lambda e: e.tensor_reduce(out=ss.ap, in_=sq.ap, axis=AX.X, op=ALU.add), [sq], [ss])
                k.V(lambda e: e.tensor_scalar(out=ss.ap, in0=ss.ap, scalar1=1.0 / 256.0, scalar2=EPS, op0=ALU.mult, op1=ALU.add), [ss], [ss])
                k.A(lambda e: e.activation(out=ss.ap, in_=ss.ap, func=AF.Ln), [ss], [ss])
                k.A(lambda e: e.activation(out=ss.ap, in_=ss.ap, func=AF.Exp, scale=-0.5), [ss], [ss])
                k.V(lambda e: e.scalar_tensor_tensor(out=of.ap, in0=of.ap, scalar=ss.ap[:, 0:1], in1=gn_ap, op0=ALU.mult, op1=ALU.mult), [of, ss, gn_buf], [of])
                oh = ogh[c % 2]
                k.V(lambda e, c=c, oh=oh: e.tensor_mul(out=oh.ap, in0=of.ap, in1=srS.ap[:, c, :]), [of, srS], [oh])
                if KCUT == 8:
                    return
                out_cb(c, oh)
        if T <= 256:
            for c in range(0, nch, 2):
                k.interleave([with_pool([0, 1, 2, 3], p2, c), with_pool([4, 5, 6, 7], p2, c + 1)])
        else:
            for c in range(nch):
                p2(c)
        return Sf, Sb

    if KSTOP in ("A", "A1"):
        xt_ = k.T([128, D], F32)
        for t in range(NT):
            k.dma(xt_.ap, xin.ap[t * 128:(t + 1) * 128, :], [xin], [xt_])
            k.dma(X2.ap[t * 128:(t + 1) * 128, :], xt_.ap, [xt_], [X2])
        k.barrier()
        k.top = PERSIST_TOP
    else:
        gla_layer()
    if KSTOP in ("G", "G1", "G0"):
        xt = k.T([128, D], F32)
        tl = list(range(NT)) if KSTOP == "G" else ([0, 1] if KSTOP == "G0" else [0, 1] + list(range(64, 80)))
        for t in tl:
            k.dma(xt.ap, X1.ap[t * 128:(t + 1) * 128, :], [X1], [xt])
            k.dma(yout.ap[t * 128:(t + 1) * 128, :], xt.ap, [xt], [yout], is_out=True)
        return k.finish()
    ustrict = din_late("ustrict", [128, 128])
    basein = din_late("basein", [128, 32])

    def moe_layer(layer, out_dram):
        k.barrier()
        k.top = PERSIST_TOP
        GRP = ((0, NTP, 1024.0), (NTP, NT, 256.0))
        us = k.T([128, 128], F32, "us"); ones = k.T([128, 128], F32, "ones")
        k.dma(us.ap, ustrict.ap, [ustrict], [us])
        k.V(lambda e: e.memset(ones.ap, 1.0), [], [ones])
        iot = k.T([128, 2], F32, "iot")
        k.dma(iot.ap, iota_in.ap, [iota_in], [iot])
        baseT = k.T([128, 2, 16], F32, "baseT")
        k.dma(baseT.ap, basein.ap.rearrange("p (g e) -> p g e", e=16), [basein], [baseT])
        idxI = k.T([128, NT, 16], I32, "idxI")
        ROUTE_TOP = k.top
        lo = k.T([128, 2, 16], F32, "lo"); hi = k.T([128, 2, 16], F32, "hi"); mid = k.T([128, 2, 16], F32, "mid")
        Cv = k.T([128, 2, 16], F32, "Cv"); cntp = k.T([128, 2, 16], F32, "cntp")
        gem = k.T([128, 2, 16], U32, "gem"); ltm = k.T([128, 2, 16], U32, "ltm")
        cmp = k.T([128, NT, 16], F32, "cmp")
        k.V(lambda e: e.memset(lo.ap, 0.0), [], [lo])
        k.V(lambda e: e.memset(hi.ap, 1.0), [], [hi])
        for g, (t0, t1, C) in enumerate(GRP):
            k.V(lambda e, g=g, C=C: e.memset(Cv.ap[:, g, :], C), [], [Cv])
        NIT = int(os.environ.get("KNIT", "30"))
        for it in range(NIT):
            k.V(lambda e: e.tensor_add(out=mid.ap, in0=lo.ap, in1=hi.ap), [lo, hi], [mid])
            k.V(lambda e: e.tensor_scalar(out=mid.ap, in0=mid.ap, scalar1=0.5, scalar2=None, op0=ALU.mult), [mid], [mid])
            for g, (t0, t1, C) in enumerate(GRP):
                nt = t1 - t0
                k.V(lambda e, g=g, t0=t0, t1=t1, nt=nt: e.tensor_tensor(out=cmp.ap[:, t0:t1, :], in0=affA.ap[:, t0:t1, :],
                                                                        in1=mid.ap[:, g, :].unsqueeze(1).to_broadcast([128, nt, 16]), op=ALU.is_ge), [affA, mid], [cmp])
                k.V(lambda e, g=g, t0=t0, t1=t1: e.tensor_reduce(out=cntp.ap[:, g, :], in_=cmp.ap[:, t0:t1, :].rearrange("p t e -> p e t"), axis=AX.X, op=ALU.add), [cmp], [cntp])
            pt = nps()
            k.mm(pt.ap[:, 0:32], ones.ap, cntp.ap.rearrange("p g e -> p (g e)"), True, True, [ones, cntp], [pt])
            k.V(lambda e, pt=pt: e.tensor_tensor(out=gem.ap.rearrange("p g e -> p (g e)"), in0=pt.ap[:, 0:32], in1=Cv.ap.rearrange("p g e -> p (g e)"), op=ALU.is_ge), [pt, Cv], [gem])
            k.V(lambda e, pt=pt: e.tensor_tensor(out=ltm.ap.rearrange("p g e -> p (g e)"), in0=pt.ap[:, 0:32], in1=Cv.ap.rearrange("p g e -> p (g e)"), op=ALU.is_lt), [pt, Cv], [ltm])
            k.V(lambda e: e.copy_predicated(out=lo.ap, mask=gem.ap, data=mid.ap), [gem, mid, lo], [lo])
            k.V(lambda e: e.copy_predicated(out=hi.ap, mask=ltm.ap, data=mid.ap), [ltm, mid, hi], [hi])
        mask = cmp
        within = k.T([128, NT, 16], F32, "within"); cntS = k.T([128, NT, 16], F32, "cntS"); incl = k.T([128, NT, 16], F32, "incl")
        onesr = k.T([128, NT], F32, "onesr")
        k.V(lambda e: e.memset(onesr.ap, 1.0), [], [onesr])
        for g, (t0, t1, C) in enumerate(GRP):
            nt = t1 - t0
            k.V(lambda e, g=g, t0=t0, t1=t1, nt=nt: e.tensor_tensor(out=mask.ap[:, t0:t1, :], in0=affA.ap[:, t0:t1, :],
                                                                    in1=lo.ap[:, g, :].unsqueeze(1).to_broadcast([128, nt, 16]), op=ALU.is_ge), [affA, lo], [mask])
        mflat = mask.ap.rearrange("p t e -> p (t e)")
        for c0 in range(0, NT * 16, 512):
            c1 = min(NT * 16, c0 + 512)
            pw = nps(); pc = nps()
            k.mm(pw.ap[:, 0:c1 - c0], us.ap, mflat[:, c0:c1], True, True, [us, mask], [pw])
            k.mm(pc.ap[:, 0:c1 - c0], ones.ap, mflat[:, c0:c1], True, True, [ones, mask], [pc])
            k.V(lambda e, pw=pw, c0=c0, c1=c1: e.tensor_copy(out=within.ap.rearrange("p t e -> p (t e)")[:, c0:c1], in_=pw.ap[:, 0:c1 - c0]), [pw], [within])
            k.A(lambda e, pc=pc, c0=c0, c1=c1: e.copy(out=cntS.ap.rearrange("p t e -> p (t e)")[:, c0:c1], in_=pc.ap[:, 0:c1 - c0]), [pc], [cntS])
        for g, (t0, t1, C) in enumerate(GRP):
            for e_ in range(16):
                k.V(lambda e, t0=t0, t1=t1, e_=e_: e.tensor_tensor_scan(out=incl.ap[:, t0:t1, e_], data0=onesr.ap[:, t0:t1], data1=cntS.ap[:, t0:t1, e_],
                                                                        initial=0.0, op0=ALU.mult, op1=ALU.add), [onesr, cntS], [incl])
        pos = within
        k.V(lambda e: e.tensor_add(out=pos.ap, in0=within.ap, in1=incl.ap), [within, incl], [pos])
        k.V(lambda e: e.tensor_sub(out=pos.ap, in0=pos.ap, in1=cntS.ap), [pos, cntS], [pos])
        sel = incl
        for g, (t0, t1, C) in enumerate(GRP):
            nt = t1 - t0
            k.V(lambda e, t0=t0, t1=t1, C=C: e.tensor_scalar(out=sel.ap[:, t0:t1, :], in0=pos.ap[:, t0:t1, :], scalar1=C, scalar2=None, op0=ALU.is_lt), [pos], [sel])
            k.V(lambda e, g=g, t0=t0, t1=t1, nt=nt: e.tensor_add(out=pos.ap[:, t0:t1, :], in0=pos.ap[:, t0:t1, :],
                                                                 in1=baseT.ap[:, g, :].unsqueeze(1).to_broadcast([128, nt, 16])), [pos, baseT], [pos])
        k.V(lambda e: e.tensor_mul(out=sel.ap, in0=sel.ap, in1=mask.ap), [sel, mask], [sel])
        oob = k.T([128, 1], F32, "oob")
        k.V(lambda e: e.tensor_scalar_add(out=oob.ap, in0=iot.ap[:, 0:1], scalar1=30000.0), [iot], [oob])
        k.V(lambda e: e.tensor_scalar(out=pos.ap, in0=pos.ap, scalar1=oob.ap[:, 0:1], scalar2=None, op0=ALU.subtract), [pos, oob], [pos])
        k.V(lambda e: e.tensor_mul(out=pos.ap, in0=pos.ap, in1=sel.ap), [pos, sel], [pos])
        k.V(lambda e: e.tensor_scalar(out=pos.ap, in0=pos.ap, scalar1=oob.ap[:, 0:1], scalar2=None, op0=ALU.add), [pos, oob], [pos])
        k.V(lambda e: e.tensor_copy(out=idxI.ap, in_=pos.ap), [pos], [idxI])
        k.barrier()
        k.top = ROUTE_TOP
        if KSTOP == "R":
            return
        XEe = [Buf(XE.ap[e_ * 1280:(e_ + 1) * 1280, :], f"XE{e_}") for e_ in range(16)]
        h2t = [k.T([128, D], BF16, f"h2t{i}") for i in range(3)]
        h2i = [0]

        def dispatch(e_):
            for t in range(NT):
                ht = h2t[h2i[0] % 3]; h2i[0] += 1
                k.dma(ht.ap, H2.ap[t * 128:(t + 1) * 128, :], [H2], [ht])
                k.emit("pool", lambda e, ht=ht, t=t, e_=e_: e.indirect_dma_start(
                    out=XE.ap, out_offset=bass.IndirectOffsetOnAxis(ap=idxI.ap[:, t, e_:e_ + 1], axis=0),
                    in_=ht.ap, in_offset=None, bounds_check=k.bc_reg, oob_is_err=False), [ht, idxI], [XEe[e_]], dma=True)
        if KSTOP == "D":
            for e_ in range(16):
                dispatch(e_)
            k.barrier()
            k.top = ROUTE_TOP
            return
        xet = [k.T([128, D], BF16, f"xet{i}") for i in range(2)]
        xeT = k.T([128, 8, 1280], BF16, "xeT")
        hidT = k.T([128, 16, 1280], BF16, "hidT")
        wst = [k.T([128, 2048], F32, f"wst{i}") for i in range(2)]
        wgb = [k.T([128, 8, 512], BF16, f"wgb{i}") for i in range(2)]
        wub = [k.T([128, 8, 512], BF16, f"wub{i}") for i in range(2)]
        wdb = k.T([128, 16, 1024], BF16, "wdb")
        sg_t = k.T([128, 512], F32, "sg_t")
        yet = [k.T([128, D], F32, f"yet{i}") for i in range(2)]
        nexp = 16 if KSTOP != "F1" else 1
        wi = [0]

        def load_cast(dst_ap, dst_buf, src_ap, src_buf, a, b, eng="act"):
            st = wst[wi[0] % 2]; wi[0] += 1
            v = st.ap[:, 0:a * b].rearrange("p (a b) -> p a b", b=b)
            k.dma(v, src_ap, [src_buf], [st])
            if eng == "act":
                k.A(lambda e: e.copy(out=dst_ap, in_=v), [st], [dst_buf])
            else:
                k.V(lambda e: e.tensor_copy(out=dst_ap, in_=v), [st], [dst_buf])

        dispatch(0)
        for e_ in range(nexp):
            if e_ + 1 < nexp:
                dispatch(e_ + 1)
            for sb in range(10):
                xt_ = xet[sb % 2]
                k.dma(xt_.ap, XE.ap[e_ * 1280 + sb * 128:e_ * 1280 + (sb + 1) * 128, :], [XEe[e_]], [xt_])
                for half in range(2):
                    p = nps(); pv = p.ap.bitcast(BF16)
                    for kk in range(4):
                        kc = half * 4 + kk
                        k.tr(pv[:, kk * 128:(kk + 1) * 128], xt_.ap[:, kc * 128:(kc + 1) * 128], identb.ap, [xt_, identb], [p])
                    k.A(lambda e, pv=pv, half=half, sb=sb: e.copy(out=xeT.ap[:, half * 4:half * 4 + 4, sb * 128:(sb + 1) * 128],
                                                                  in_=pv[:, 0:512].rearrange("p (a b) -> p a b", b=128)), [p], [xeT])
            for fb in range(4):
                wg_ = wgb[fb % 2]; wu_ = wub[fb % 2]
                for kh in range(2):
                    load_cast(wg_.ap[:, kh * 4:(kh + 1) * 4, :], wg_, wg.ap[layer, e_, kh * 512:(kh + 1) * 512, fb * 512:(fb + 1) * 512].rearrange("(k p) c -> p k c", p=128), wg, 4, 512)
                    load_cast(wu_.ap[:, kh * 4:(kh + 1) * 4, :], wu_, wu.ap[layer, e_, kh * 512:(kh + 1) * 512, fb * 512:(fb + 1) * 512].rearrange("(k p) c -> p k c", p=128), wu, 4, 512)
                for f4 in range(4):
                    fc = fb * 4 + f4
                    for (s0, s1) in ((0, 512), (512, 1024), (1024, 1280)):
                        pg_ = nps(); pu_ = nps()
                        for kk in range(8):
                            k.mm(pg_.ap[:, 0:s1 - s0], wg_.ap[:, kk, f4 * 128:(f4 + 1) * 128], xeT.ap[:, kk, s0:s1], kk == 0, kk == 7, [wg_, xeT], [pg_])
                        for kk in range(8):
                            k.mm(pu_.ap[:, 0:s1 - s0], wu_.ap[:, kk, f4 * 128:(f4 + 1) * 128], xeT.ap[:, kk, s0:s1], kk == 0, kk == 7, [wu_, xeT], [pu_])
                        k.A(lambda e, pg_=pg_, s0=s0, s1=s1: e.activation(out=sg_t.ap[:, 0:s1 - s0], in_=pg_.ap[:, 0:s1 - s0], func=AF.Silu), [pg_], [sg_t])
                        k.V(lambda e, pu_=pu_, s0=s0, s1=s1, fc=fc: e.tensor_mul(out=hidT.ap[:, fc, s0:s1], in0=sg_t.ap[:, 0:s1 - s0], in1=pu_.ap[:, 0:s1 - s0]), [sg_t, pu_], [hidT])
            for fb in range(8):
                load_cast(wdb.ap[:, fb * 2:(fb + 1) * 2, :], wdb, wd.ap[layer, e_, fb * 256:(fb + 1) * 256, :].rearrange("(k p) c -> p k c", p=128), wd, 2, 1024, eng="dve")
            for sb in range(10):
                yt_ = yet[sb % 2]
                for half in range(2):
                    py = nps()
                    for fc in range(16):
                        k.mm(py.ap, hidT.ap[:, fc, sb * 128:(sb + 1) * 128], wdb.ap[:, fc, half * 512:(half + 1) * 512], fc == 0, fc == 15, [hidT, wdb], [py])
                    k.A(lambda e, py=py, half=half, yt_=yt_: e.copy(out=yt_.ap[:, half * 512:(half + 1) * 512], in_=py.ap), [py], [yt_])
                k.dma(YE.ap[e_ * 1280 + sb * 128:e_ * 1280 + (sb + 1) * 128, :], yt_.ap, [yt_], [YE])
        k.barrier()
        k.top = ROUTE_TOP
        if KSTOP in ("F", "F1"):
            return
        Gt = [k.T([128, 16, D], F32, f"Gt{i}") for i in range(2)]
        xts = [k.T([128, D], F32, f"cx{i}") for i in range(2)]
        acc = k.T([128, D], F32, "acc")
        gaT = k.T([128, D]); lgT = k.T([128, D]); lbT = k.T([128, D])
        bcast_row(lgT, lng, layer * 2 + 1); bcast_row(lbT, lnb, layer * 2 + 1)
        cur = None
        for t in range(NT):
            row = row_of_tile(t)
            if row != cur:
                cur = row
                get_mod(gaT, layer, 5, row)
            G_ = Gt[t % 2]; xt = xts[t % 2]
            k.A(lambda e, G_=G_: e.memzero(G_.ap), [], [G_])
            for e_ in range(16):
                k.emit("pool", lambda e, G_=G_, t=t, e_=e_: e.indirect_dma_start(
                    out=G_.ap[:, e_, :], out_offset=None, in_=YE.ap,
                    in_offset=bass.IndirectOffsetOnAxis(ap=idxI.ap[:, t, e_:e_ + 1], axis=0),
                    bounds_check=k.bc_reg, oob_is_err=False), [YE, idxI], [G_], dma=True)
            k.dma(xt.ap, X1.ap[t * 128:(t + 1) * 128, :], [X1], [xt])
            k.V(lambda e, G_=G_, t=t: e.tensor_scalar(out=acc.ap, in0=G_.ap[:, 0, :], scalar1=affA.ap[:, t, 0:1], scalar2=None, op0=ALU.mult), [G_, affA], [acc])
            for e_ in range(1, 16):
                k.V(lambda e, G_=G_, t=t, e_=e_: e.scalar_tensor_tensor(out=acc.ap, in0=G_.ap[:, e_, :], scalar=affA.ap[:, t, e_:e_ + 1], in1=acc.ap,
                                                                        op0=ALU.mult, op1=ALU.add), [G_, affA, acc], [acc])
            k.V(lambda e: e.tensor_mul(out=acc.ap, in0=acc.ap, in1=gaT.ap), [acc, gaT], [acc])
            k.V(lambda e, xt=xt: e.scalar_tensor_tensor(out=xt.ap, in0=xt.ap, scalar=ALPHA, in1=acc.ap, op0=ALU.mult, op1=ALU.add), [xt, acc], [xt])
            layernorm(xt, lgT, lbT)
            k.dma(out_dram.ap[t * 128:(t + 1) * 128, :], xt.ap, [xt], [out_dram], is_out=(out_dram is yout))
        k.barrier()
        k.top = PERSIST_TOP

    if KSTOP != "MOEONLY":
        pass
    if KSTOP not in ("A", "A1"):
        moe_layer(0, X2 if KSTOP not in ("R", "D", "F", "F1", "C") else yout)
    if KSTOP in ("F", "F1"):
        xt = k.T([128, D], F32)
        for t in range(10):
            k.dma(xt.ap, YE.ap[t * 128:(t + 1) * 128, :], [YE], [xt])
            k.dma(yout.ap[t * 128:(t + 1) * 128, :], xt.ap, [xt], [yout], is_out=True)
    if KSTOP in ("R", "D", "F", "F1", "C"):
        return k.finish()
    def attn_layer():
        k.barrier()
        k.top = PERSIST_TOP
        LY = 1
        stage = [k.T([128, 1024], F32, f"stg{i}") for i in range(2)]
        Wa = k.T([128, 8, 1536], BF16, "Wa"); Wao = k.T([128, 8, 1024], BF16, "Wao")
        gqT = k.T([128, 128], F32); gkT = k.T([128, 128], F32)
        scT = k.T([128, D]); shT = k.T([128, D]); tmpT = k.T([128, D])
        wrT = k.T([128, 8, 16], F32)
        h2bT = k.T([128, D], BF16); h2TT = k.T([128, 8, 128], F32)
        xts = [k.T([128, D], F32, f"xt{i}") for i in range(2)]
        hb = k.T([128, D], BF16, "hb")
        load_w_bf16(Wa, wa_in.ap, wa_in, 8, 1536, stage)
        load_w_bf16(Wao, wa_out.ap, wa_out, 8, 1024, stage)
        bcast_row(gqT, gqk, 0); bcast_row(gkT, gqk, 1)
        k.dma(wrT.ap, w_router.ap[LY].rearrange("(k p) c -> p k c", p=128), [w_router], [wrT])
        BASE_TOP = k.top

        def alloc_post(row):
            P = dict(ga=k.T([128, D]), sc2=k.T([128, D]), sh2=k.T([128, D]), lg=k.T([128, D]), lb=k.T([128, D]))
            get_mod(P["ga"], LY, 2, row); get_mod(P["sc2"], LY, 4, row, plus1=True); get_mod(P["sh2"], LY, 3, row)
            bcast_row(P["lg"], lng, LY * 2); bcast_row(P["lb"], lnb, LY * 2)
            return P

        def run_seqs(seqs, T, nctx):
            ntile = T // 128
            Tk = T + nctx
            nkt = Tk // 128
            CH = min(T, 512)
            hT = k.T([128, 8, T], BF16, "hT")
            oS_ap = hT.ap.rearrange("p a t -> p (a t)").rearrange("p (n c) -> p n c", c=1024)
            oT = k.T([128, 8, 128], BF16, "oT")
            SCR_TOP = k.top
            qT = k.T([128, 8, T], BF16, "qT")
            kT = k.T([128, 2, Tk], BF16, "kT")
            vA = k.T([128, nkt, 2, 130], BF16, "vA")
            PT = k.T([128, nkt, CH], BF16, "PT")
            qf = k.T([128, D], F32, "qf"); qr = k.T([128, D], F32, "qr"); sq = k.T([128, D], F32, "sq")
            kvf = k.T([128, 512], F32, "kvf"); kr = k.T([128, 256], F32, "kr")
            qb = k.T([128, D], BF16, "qb"); kb = k.T([128, 256], BF16, "kb")
            ss = k.T([128, 16], F32, "ss"); rc = k.T([128, 1], F32, "rc")
            ropeT = k.T([128, ntile, 2, 64], F32, "ropeT") if nctx else None
            cst = k.T([128, 128], F32, "cst") if nctx else None
            cur_row = [None]
            for (t0, row, so, bidx) in seqs:
                if cur_row[0] != row:
                    cur_row[0] = row
                    get_mod(scT, LY, 1, row, plus1=True); get_mod(shT, LY, 0, row)
                k.V(lambda e: e.memset(vA.ap, 1.0), [], [vA])
                if nctx:
                    k.dma(ropeT.ap, rope.ap.rearrange("(n p) a c -> p n a c", p=128), [rope], [ropeT])
                if nctx:
                    for kvh in range(2):
                        for j in range(2):
                            k.dma(cst.ap, ck.ap[bidx, kvh, j * 128:(j + 1) * 128, :], [ck], [cst])
                            k.V(lambda e: e.tensor_copy(out=kb.ap[:, 0:128], in_=cst.ap), [cst], [kb])
                            p = nps(); pv = p.ap.bitcast(BF16)
                            k.tr(pv[:, 0:128], kb.ap[:, 0:128], identb.ap, [kb, identb], [p])
                            k.A(lambda e, pv=pv, kvh=kvh, j=j: e.copy(out=kT.ap[:, kvh, j * 128:(j + 1) * 128], in_=pv[:, 0:128]), [p], [kT])
                            k.dma(cst.ap, cv_.ap[bidx, kvh, j * 128:(j + 1) * 128, :], [cv_], [cst])
                            k.V(lambda e, kvh=kvh, j=j: e.tensor_copy(out=vA.ap[:, j, kvh, 0:128], in_=cst.ap), [cst], [vA])
                for ti in range(ntile):
                    xt = xts[ti % 2]
                    k.dma(xt.ap, X2.ap[(t0 + ti) * 128:(t0 + ti + 1) * 128, :], [X2], [xt])
                    k.V(lambda e, xt=xt: e.tensor_mul(out=tmpT.ap, in0=xt.ap, in1=scT.ap), [xt, scT], [tmpT])
                    k.V(lambda e: e.tensor_add(out=hb.ap, in0=tmpT.ap, in1=shT.ap), [tmpT, shT], [hb])
                    for half in range(2):
                        p = nps(); pv = p.ap.bitcast(BF16)
                        for kk in range(4):
                            kc = half * 4 + kk
                            k.tr(pv[:, kk * 128:(kk + 1) * 128], hb.ap[:, kc * 128:(kc + 1) * 128], identb.ap, [hb, identb], [p])
                        k.A(lambda e, pv=pv, half=half, ti=ti: e.copy(out=hT.ap[:, half * 4:half * 4 + 4, ti * 128:(ti + 1) * 128],
                                                                      in_=pv[:, 0:512].rearrange("p (a b) -> p a b", b=128)), [p], [hT])
                for ti in range(ntile):
                    tsl = slice(ti * 128, (ti + 1) * 128)
                    for half in range(2):
                        pq = nps()
                        for kk in range(8):
                            k.mm(pq.ap, hT.ap[:, kk, tsl], Wa.ap[:, kk, half * 512:(half + 1) * 512], kk == 0, kk == 7, [hT, Wa], [pq])
                        k.A(lambda e, pq=pq, half=half: e.copy(out=qf.ap[:, half * 512:(half + 1) * 512], in_=pq.ap), [pq], [qf])
                    pk = nps()
                    for kk in range(8):
                        k.mm(pk.ap, hT.ap[:, kk, tsl], Wa.ap[:, kk, 1024:1536], kk == 0, kk == 7, [hT, Wa], [pk])
                    k.A(lambda e, pk=pk: e.copy(out=kvf.ap, in_=pk.ap), [pk], [kvf])
                    k.V(lambda e: e.tensor_mul(out=sq.ap, in0=qf.ap, in1=qf.ap), [qf], [sq])
                    k.V(lambda e: e.tensor_reduce(out=ss.ap[:, 0:8], in_=sq.ap.rearrange("p (h d) -> p h d", d=128), axis=AX.X, op=ALU.add), [sq], [ss])
                    k.V(lambda e: e.tensor_mul(out=sq.ap[:, 0:256], in0=kvf.ap[:, 0:256], in1=kvf.ap[:, 0:256]), [kvf], [sq])
                    k.V(lambda e: e.tensor_reduce(out=ss.ap[:, 8:10], in_=sq.ap[:, 0:256].rearrange("p (h d) -> p h d", d=128), axis=AX.X, op=ALU.add), [sq], [ss])
                    k.V(lambda e: e.tensor_scalar(out=ss.ap[:, 0:10], in0=ss.ap[:, 0:10], scalar1=1.0 / 128.0, scalar2=EPS, op0=ALU.mult, op1=ALU.add), [ss], [ss])
                    k.A(lambda e: e.activation(out=ss.ap[:, 0:10], in_=ss.ap[:, 0:10], func=AF.Sqrt), [ss], [ss])
                    k.V(lambda e: e.reciprocal(out=ss.ap[:, 0:10], in_=ss.ap[:, 0:10]), [ss], [ss])
                    k.V(lambda e: e.tensor_mul(out=qf.ap.rearrange("p (h d) -> p h d", d=128), in0=qf.ap.rearrange("p (h d) -> p h d", d=128),
                                               in1=ss.ap[:, 0:8].unsqueeze(2).to_broadcast([128, 8, 128])), [qf, ss], [qf])
                    k.V(lambda e: e.tensor_mul(out=qf.ap.rearrange("p (h d) -> p h d", d=128), in0=qf.ap.rearrange("p (h d) -> p h d", d=128),
                                               in1=gqT.ap.unsqueeze(1).to_broadcast([128, 8, 128])), [qf, gqT], [qf])
                    k.V(lambda e: e.tensor_mul(out=kvf.ap[:, 0:256].rearrange("p (h d) -> p h d", d=128), in0=kvf.ap[:, 0:256].rearrange("p (h d) -> p h d", d=128),
                                               in1=ss.ap[:, 8:10].unsqueeze(2).to_broadcast([128, 2, 128])), [kvf, ss], [kvf])
                    k.V(lambda e: e.tensor_mul(out=kvf.ap[:, 0:256].rearrange("p (h d) -> p h d", d=128), in0=kvf.ap[:, 0:256].rearrange("p (h d) -> p h d", d=128),
                                               in1=gkT.ap.unsqueeze(1).to_broadcast([128, 2, 128])), [kvf, gkT], [kvf])
                    if so is not None:
                        for kvh in range(2):
                            k.dma(nk.ap[so, kvh, tsl, :], kvf.ap[:, kvh * 128:(kvh + 1) * 128], [kvf], [nk], is_out=True)
                            k.dma(nv.ap[so, kvh, tsl, :], kvf.ap[:, 256 + kvh * 128:256 + (kvh + 1) * 128], [kvf], [nv], is_out=True)
                    if nctx:
                        for (src, dst, H) in ((qf, qr, 8), (kvf, kr, 2)):
                            sv = src.ap[:, 0:H * 128].rearrange("p (h a f g) -> p h a f g", a=2, f=2, g=32)
                            dv = dst.ap[:, 0:H * 128].rearrange("p (h a f g) -> p h a f g", a=2, f=2, g=32)
                            tv = sq.ap[:, 0:H * 128].rearrange("p (h a f g) -> p h a f g", a=2, f=2, g=32)
                            cosb = ropeT.ap[:, ti, 0, :].rearrange("p (a g) -> p a g", g=32).unsqueeze(1).to_broadcast([128, H, 2, 32])
                            sinb = ropeT.ap[:, ti, 1, :].rearrange("p (a g) -> p a g", g=32).unsqueeze(1).to_broadcast([128, H, 2, 32])
                            x1 = sv[:, :, :, 0, :]; x2 = sv[:, :, :, 1, :]
                            k.V(lambda e, x1=x1, cosb=cosb, dv=dv: e.tensor_mul(out=dv[:, :, :, 0, :], in0=x1, in1=cosb), [src, ropeT], [dst])
                            k.V(lambda e, x2=x2, sinb=sinb, tv=tv: e.tensor_mul(out=tv[:, :, :, 0, :], in0=x2, in1=sinb), [src, ropeT], [sq])
                            k.V(lambda e, dv=dv, tv=tv: e.tensor_sub(out=dv[:, :, :, 0, :], in0=dv[:, :, :, 0, :], in1=tv[:, :, :, 0, :]), [dst, sq], [dst])
                            k.V(lambda e, x2=x2, cosb=cosb, dv=dv: e.tensor_mul(out=dv[:, :, :, 1, :], in0=x2, in1=cosb), [src, ropeT], [dst])
                            k.V(lambda e, x1=x1, sinb=sinb, tv=tv: e.tensor_mul(out=tv[:, :, :, 1, :], in0=x1, in1=sinb), [src, ropeT], [sq])
                            k.V(lambda e, dv=dv, tv=tv: e.tensor_add(out=dv[:, :, :, 1, :], in0=dv[:, :, :, 1, :], in1=tv[:, :, :, 1, :]), [dst, sq], [dst])
                        qsrc, ksrc = qr, kr
                    else:
                        qsrc, ksrc = qf, kvf
                    k.A(lambda e, qsrc=qsrc: e.copy(out=qb.ap, in_=qsrc.ap), [qsrc], [qb])
                    k.A(lambda e, ksrc=ksrc: e.copy(out=kb.ap, in_=ksrc.ap[:, 0:256]), [ksrc], [kb])
                    for kvh in range(2):
                        k.V(lambda e, kvh=kvh, ti=ti: e.tensor_copy(out=vA.ap[:, nctx // 128 + ti, kvh, 0:128], in_=kvf.ap[:, 256 + kvh * 128:256 + (kvh + 1) * 128]), [kvf], [vA])
                    p = nps(); pv = p.ap.bitcast(BF16)
                    for h in range(8):
                        k.tr(pv[:, h * 128:(h + 1) * 128], qb.ap[:, h * 128:(h + 1) * 128], identb.ap, [qb, identb], [p])
                    k.A(lambda e, pv=pv, tsl=tsl: e.copy(out=qT.ap[:, :, tsl], in_=pv[:, 0:1024].rearrange("p (a b) -> p a b", b=128)), [p], [qT])
                    p = nps(); pv = p.ap.bitcast(BF16)
                    for kvh in range(2):
                        k.tr(pv[:, kvh * 128:(kvh + 1) * 128], kb.ap[:, kvh * 128:(kvh + 1) * 128], identb.ap, [kb, identb], [p])
                    k.A(lambda e, pv=pv, ti=ti: e.copy(out=kT.ap[:, :, nctx + ti * 128:nctx + (ti + 1) * 128], in_=pv[:, 0:256].rearrange("p (a b) -> p a b", b=128)), [p], [kT])
                for h in range(8):
                    kvh = h // 4
                    for c0 in range(0, T, CH):
                        for kt in range(nkt):
                            ps_ = nps()
                            k.mm(ps_.ap[:, 0:CH], kT.ap[:, kvh, kt * 128:(kt + 1) * 128], qT.ap[:, h, c0:c0 + CH], True, True, [kT, qT], [ps_])
                            k.A(lambda e, ps_=ps_, kt=kt: e.activation(out=PT.ap[:, kt, :], in_=ps_.ap[:, 0:CH], func=AF.Exp, scale=128.0 ** -0.5), [ps_], [PT])
                        for tq in range(CH // 128):
                            po = nps()
                            for kt in range(nkt):
                                k.mm(po.ap[:, 0:129], PT.ap[:, kt, tq * 128:(tq + 1) * 128], vA.ap[:, kt, kvh, 0:129], kt == 0, kt == nkt - 1, [PT, vA], [po])
                            k.V(lambda e, po=po: e.reciprocal(out=rc.ap, in_=po.ap[:, 128:129]), [po], [rc])
                            tile_i = c0 // 128 + tq
                            k.V(lambda e, po=po, tile_i=tile_i, h=h: e.tensor_scalar(out=oS_ap[:, tile_i, h * 128:(h + 1) * 128], in0=po.ap[:, 0:128], scalar1=rc.ap[:, 0:1],
                                                                                      scalar2=None, op0=ALU.mult), [po, rc], [hT])
                k.barrier()
                k.top = SCR_TOP
                PP = alloc_post(row)
                for ti in range(ntile):
                    xt = xts[ti % 2]
                    k.dma(xt.ap, X2.ap[(t0 + ti) * 128:(t0 + ti + 1) * 128, :], [X2], [xt])
                    p = nps(); pv = p.ap.bitcast(BF16)
                    for kk in range(8):
                        k.tr(pv[:, kk * 128:(kk + 1) * 128], oS_ap[:, ti, kk * 128:(kk + 1) * 128], identb.ap, [hT, identb], [p])
                    k.A(lambda e, pv=pv: e.copy(out=oT.ap, in_=pv[:, 0:1024].rearrange("p (a b) -> p a b", b=128)), [p], [oT])
                    yps = []
                    for half in range(2):
                        py = nps()
                        for kk in range(8):
                            k.mm(py.ap, oT.ap[:, kk, :], Wao.ap[:, kk, half * 512:(half + 1) * 512], kk == 0, kk == 7, [oT, Wao], [py])
                        yps.append(py)
                    post_mixer(t0 + ti, xt, yps, PP["ga"], PP["lg"], PP["lb"], PP["sc2"], PP["sh2"], wrT, tmpT, h2bT, h2TT)
                k.barrier()

        k.top = BASE_TOP
        nseq = 32 if KSTOP != "A1" else 1
        run_seqs([(2 * s, 0, s, None) for s in range(nseq)], 256, 0)
        k.barrier()
        k.top = BASE_TOP
        run_seqs([(NTP + 8 * b, 1 + b, None, b) for b in range(2)], 1024, 256)
        k.barrier()
        k.top = PERSIST_TOP

    if KSTOP in ("A", "A1"):
        pass
    attn_layer()
    if KSTOP in ("A", "A1"):
        xt = k.T([128, D], F32)
        tl = list(range(NT)) if KSTOP == "A" else [0, 1] + list(range(64, 80))
        for t in tl:
            k.dma(xt.ap, X1.ap[t * 128:(t + 1) * 128, :], [X1], [xt])
            k.dma(yout.ap[t * 128:(t + 1) * 128, :], xt.ap, [xt], [yout], is_out=True)
        return k.finish()
    moe_layer(1, yout)

    return k.finish()


def _consts():
    s = np.arange(64)[:, None]
    t = np.arange(64)[None, :]
    c = -1.0 / 16.0
    cm = np.zeros((64, 6, 64), np.float32)
    cm[:, 0, :] = c * (s <= t)
    cm[:, 1, :] = c * (s > t)
    cm[:, 2, :] = c * (s >= t)
    cm[:, 3, :] = c * (s < t)
    cm[:, 4, :] = 1.0 * (t >= s)
    cm[:, 5, :] = 1.0 * (t <= s)
    return cm


def _rope_tables():
    T = 1024; W = 64
    rows = T // W
    row = np.repeat(np.arange(rows), W).astype(np.float32)
    col = np.tile(np.arange(W), rows).astype(np.float32)
    inv = (10000.0 ** (-np.arange(0, 64, 2, dtype=np.float32) / 64.0)).astype(np.float32)
    ang = np.stack([row[:, None] * inv, col[:, None] * inv], axis=1).astype(np.float32)
    cos = np.cos(ang).astype(np.float32).reshape(T, 64); sin = np.sin(ang).astype(np.float32).reshape(T, 64)
    return np.stack([cos, sin], axis=1).astype(np.float32)


def make_in_map(inp):
    f = lambda a: np.ascontiguousarray(np.asarray(a, dtype=np.float32))
    m = {}
    m["xin"] = f(np.concatenate([f(inp["x_prompt"]).reshape(-1, D), f(inp["x_sample"]).reshape(-1, D)], axis=0))
    cv = np.stack([f(inp["c_ctx"]), f(inp["c"])[0], f(inp["c"])[1]], axis=0)
    m["cvT"] = f(cv.T.reshape(8, 128, 3).transpose(1, 0, 2))
    m["w_mod"] = f(inp["w_mod"]); m["b_mod"] = f(inp["b_mod"])
    m["lng"] = f(inp["ln_g"]).reshape(4, D); m["lnb"] = f(inp["ln_b"]).reshape(4, D)
    wi = f(inp["w_gla_in"])[0]
    hd = lambda hh: np.concatenate([wi[:, hh * 128:(hh + 1) * 128], wi[:, 512 + hh * 128:512 + (hh + 1) * 128],
                                    wi[:, 1024 + hh * 256:1024 + (hh + 1) * 256], wi[:, 2048 + hh * 256:2048 + (hh + 1) * 256]], axis=1)
    m["w_in_hd"] = f(np.concatenate([hd(hh) for hh in range(4)], axis=1))
    m["w_g1"] = f(np.concatenate([f(inp["w_gla_gf1"])[0], f(inp["w_gla_gb1"])[0]], axis=1))
    m["w_gf2a"] = f(np.concatenate([f(inp["w_gla_gf2"])[0], f(inp["b_gla_gf"])[0][None, :]], axis=0))
    m["w_gb2a"] = f(np.concatenate([f(inp["w_gla_gb2"])[0], f(inp["b_gla_gb"])[0][None, :]], axis=0))
    m["gnorm"] = f(inp["g_gla_norm"])[0][None, :]
    m["w_out"] = f(inp["w_gla_out"])[0]
    m["S0f"] = f(inp["state_gla_fwd"])[:, 0]; m["S0b"] = f(inp["state_gla_bwd"])[:, 0]
    m["cmask"] = _consts(); m["identin"] = np.eye(128, dtype=np.float32)
    m["w_router"] = f(inp["w_router"])
    m["wg"] = f(inp["w_moe_gate"]); m["wu"] = f(inp["w_moe_up"]); m["wd"] = f(inp["w_moe_down"])
    io = np.zeros((128, 2), np.float32); io[:, 0] = np.arange(128)
    m["iota_in"] = io
    m["wa_in"] = f(inp["w_attn_in"])[0]; m["wa_out"] = f(inp["w_attn_out"])[0]
    m["gqk"] = f(np.stack([f(inp["g_attn_q"])[0], f(inp["g_attn_k"])[0]], axis=0))
    m["ck"] = f(inp["cache_attn_k"])[:, 0]; m["cv_"] = f(inp["cache_attn_v"])[:, 0]
    m["rope"] = _rope_tables()
    pp = np.arange(128)
    m["ustrict"] = (pp[:, None] < pp[None, :]).astype(np.float32)
    be = np.zeros((128, 2, 16), np.float32)
    be[:, 0, :] = np.arange(16)[None, :] * 1280.0
    be[:, 1, :] = np.arange(16)[None, :] * 1280.0 + 1024.0
    m["basein"] = be.reshape(128, 32)
    return m


_NC = None


def kernel(**inp):
    global _NC
    if _NC is None:
        _NC = build()
    m = make_in_map(inp)
    res = run_bass_kernel_spmd(_NC, [m], core_ids=[0])
    R = res.results[0]
    y = R["yout"]
    y_prompt = y[:8192].reshape(32, 256, D)
    y_sample = y[8192:].reshape(2, 1024, D)
    nf = R["nf"][:, None]; nb = R["nb"][:, None]
    nk = R["nk"][:, None]; nv = R["nv"][:, None]
    return tuple(np.ascontiguousarray(a, dtype=np.float32) for a in (y_prompt, y_sample, nf, nb, nk, nv))
```

```python
import numpy as np
import ml_dtypes
import concourse.bass as bass
import concourse.mybir as mybir
from concourse.bass_utils import run_bass_kernel_spmd

F32 = mybir.dt.float32
BF16 = mybir.dt.bfloat16
U32 = mybir.dt.uint32
ALU = mybir.AluOpType
AF = mybir.ActivationFunctionType
AX = mybir.AxisListType

NCORE = 8
D = 1024
ALPHA = 4.0 ** 0.25
EPS = 1e-6
NDS = 24


class Buf:
    __slots__ = ("ap", "w", "r", "name", "excl")

    def __init__(self, ap, name="", excl=False):
        self.ap = ap
        self.w = None
        self.r = {}
        self.name = name
        self.excl = excl


class Eng:
    def __init__(self, name):
        self.name = name
        self.ops = []
        self.clock = {}
        self.semid = None
        self.dma_sems = []
        self.dma_i = 0


class KB:
    def __init__(self):
        self.nc = bass.Bass("TRN2", target_bir_lowering=False)
        nc = self.nc
        self.E = {n: Eng(n) for n in ("pe", "act", "dve", "pool", "sp")}
        self.sems = []
        self.semcount = []
        for n in ("pe", "act", "dve", "pool", "sp"):
            self.E[n].semid = self._newsem("e_" + n)
        for n in ("sp", "act", "pool"):
            self.E[n].dma_sems = [self._newsem(f"d_{n}{i}") for i in range(NDS)]
        self.hist = {}
        self.arena = nc.alloc_sbuf_tensor("arena", [128, ARENA_W], F32).ap()
        self.top = 0
        self.psb = [Buf(nc.alloc_psum_tensor(f"ps{i}", [128, 512], F32).ap(), f"ps{i}", excl=True) for i in range(8)]
        self.outs_ev = []
        self.ndram = 0

    def _newsem(self, name):
        self.sems.append(self.nc.alloc_semaphore(name))
        self.semcount.append(0)
        return len(self.sems) - 1

    def T(self, shape, dtype=F32, name=""):
        per = int(np.prod(shape[1:]))
        words = per if dtype in (F32, U32, mybir.dt.int32) else (per + 1) // 2
        words = (words + 7) // 8 * 8
        assert self.top + words <= ARENA_W, (name, self.top, words)
        ap = self.arena[:, self.top:self.top + words]
        if not hasattr(self, "reg"):
            self.reg = {}
        self.reg[name + "@" + str(self.top)] = (self.top, list(shape), str(dtype))
        self.top += words
        if dtype != F32:
            ap = ap.bitcast(dtype)
        ap = ap[:, 0:per]
        if shape[0] != 128:
            ap = ap[0:shape[0], :]
        if len(shape) == 3:
            ap = ap.rearrange("p (a b) -> p a b", b=shape[2])
        elif len(shape) == 4:
            ap = ap.rearrange("p (a b c) -> p a b c", b=shape[2], c=shape[3])
        return Buf(ap, name)

    def dram(self, shape, dtype=F32, name=None, kind="Internal"):
        if name is None:
            self.ndram += 1
            name = f"scr{self.ndram}"
        return Buf(self.nc.dram_tensor(name, list(shape), dtype, kind=kind).ap(), name)

    def _learn(self, eng, s, c):
        h = self.hist.get((s, c))
        if h:
            ck = eng.clock
            for k, v in h.items():
                if ck.get(k, 0) < v:
                    ck[k] = v
        if eng.clock.get(s, 0) < c:
            eng.clock[s] = c

    capture = None

    def interleave(self, thunks):
        lists = []
        for th in thunks:
            self.capture = []
            th()
            lists.append(self.capture)
            self.capture = None
        n = max((len(l) for l in lists), default=0)
        for i in range(n):
            for l in lists:
                if i < len(l):
                    self.emit(*l[i])

    def emit(self, en, fn, R, W, dma=False, cc=False):
        if self.capture is not None:
            self.capture.append((en, fn, list(R), list(W), dma, cc))
            return None
        eng = self.E[en]
        deps = {}
        for b in R:
            if b.w is not None and deps.get(b.w[0], 0) < b.w[1]:
                deps[b.w[0]] = b.w[1]
            if b.excl:
                for s, c in b.r.items():
                    if s != eng.semid and deps.get(s, 0) < c:
                        deps[s] = c
        for b in W:
            if b.w is not None and deps.get(b.w[0], 0) < b.w[1]:
                deps[b.w[0]] = b.w[1]
            for s, c in b.r.items():
                if deps.get(s, 0) < c:
                    deps[s] = c
        waits = []
        for s, c in deps.items():
            if en == "pe" and s == eng.semid:
                continue
            if eng.clock.get(s, 0) >= c:
                continue
            waits.append((s, c))
            self._learn(eng, s, c)
        if dma or cc:
            slot = eng.dma_i % NDS
            eng.dma_i += 1
            s = eng.dma_sems[slot]
            prev = self.semcount[s]
            if prev and eng.clock.get(s, 0) < prev:
                waits.append((s, prev))
                self._learn(eng, s, prev)
            inc = 1 if cc else 16
            self.semcount[s] = prev + inc
            ev = (s, prev + inc)
        else:
            s = eng.semid
            self.semcount[s] += 1
            ev = (s, self.semcount[s])
            inc = 1
        self.hist[ev] = dict(eng.clock)
        for b in R:
            if b.r.get(ev[0], 0) < ev[1]:
                b.r[ev[0]] = ev[1]
        for b in W:
            b.w = ev
            b.r = {}
        eng.ops.append((waits, fn, s, inc, cc))
        return ev

    def barrier(self):
        last = {s: c for s, c in enumerate(self.semcount) if c}
        for en, eng in self.E.items():
            waits = []
            for s, c in last.items():
                if eng.clock.get(s, 0) < c:
                    waits.append((s, c))
                    eng.clock[s] = c
            if waits:
                eng.ops.append((waits, None, None, 0, False))

    def mm(self, out, lhsT, rhs, start, stop, R, W):
        return self.emit("pe", lambda e: e.matmul(out, lhsT, rhs, start=start, stop=stop), R, W)

    def tr(self, out, in_, ident, R, W):
        return self.emit("pe", lambda e: e.transpose(out, in_, ident), R, W)

    def V(self, fn, R, W):
        return self.emit("dve", fn, R, W)

    def A(self, fn, R, W):
        return self.emit("act", fn, R, W)

    def G(self, fn, R, W):
        return self.emit("pool", fn, R, W)

    def dma(self, out, in_, R, W, q="sp", is_out=False):
        ev = self.emit(q, lambda e: e.dma_start(out=out, in_=in_), R, W, dma=True)
        if is_out:
            self.outs_ev.append(ev)
        return ev

    def coll(self, kind, op, src, dst):
        grp = [list(range(NCORE))]
        return self.emit("pool", lambda e: e.collective_compute(kind, op, replica_groups=grp,
                                                                  ins=[src.ap.opt()], outs=[dst.ap.opt()]),
                         [src], [dst], cc=True)

    def finish(self):
        eng = self.E["sp"]
        waits = []
        for (s, c) in self.outs_ev:
            if eng.clock.get(s, 0) < c:
                waits.append((s, c))
                eng.clock[s] = c
        eng.ops.append((waits, None, None, 0, False))
        self.barrier()
        nc = self.nc
        sems = self.sems
        with nc.Block() as block:
            def run(eng):
                def body(e):
                    if eng.name == "pool":
                        self.bc_reg = e.to_reg(20479)
                    for (waits, fn, s, inc, cc) in eng.ops:
                        for (ws, wc) in waits:
                            e.wait_ge(sems[ws], wc)
                        if fn is not None:
                            ins = fn(e)
                            if cc:
                                ins.then_inc(sems[s])
                            else:
                                ins.then_inc(sems[s], inc)
                return body
            block.tensor(run(self.E["pe"]))
            block.scalar(run(self.E["act"]))
            block.vector(run(self.E["dve"]))
            block.gpsimd(run(self.E["pool"]))
            block.sync(run(self.E["sp"]))
        return nc


ARENA_W = 45056


NT = 80
NTP = 64
I32 = mybir.dt.int32
import os
KSTOP = os.environ.get("KSTOP", "")
KCUT = int(os.environ.get("KCUT", "0"))
KSKIP = os.environ.get("KSKIP", "").split(",")


LASTK = [None]


def build():
    k = KB()
    LASTK[0] = k
    nc = k.nc
    PS = k.psb

    def din(name, shape, dtype=F32):
        return k.dram(shape, dtype, name=name, kind="ExternalInput")

    din_late = din

    def dout(name, shape, dtype=F32):
        return k.dram(shape, dtype, name=name, kind="ExternalOutput")

    xin = din("xin", [NT * 128, D])
    cvT = din("cvT", [128, 8, 3]); w_mod = din("w_mod", [2, 1024, 6144]); b_mod = din("b_mod", [2, 6144])
    lng = din("lng", [4, D]); lnb = din("lnb", [4, D])
    w_in_hd = din("w_in_hd", [1024, 3072]); w_g1 = din("w_g1", [1024, 32])
    w_gf2a = din("w_gf2a", [17, 512]); w_gb2a = din("w_gb2a", [17, 512])
    gnorm = din("gnorm", [1, 1024]); w_out = din("w_out", [1024, 1024])
    S0f = din("S0f", [2, 4, 128, 256]); S0b = din("S0b", [2, 4, 128, 256])
    cmask = din("cmask", [64, 6, 64]); identin = din("identin", [128, 128])
    w_router = din("w_router", [2, 1024, 16])
    wg = din("wg", [2, 16, 1024, 2048]); wu = din("wu", [2, 16, 1024, 2048]); wd = din("wd", [2, 16, 2048, 1024])
    iota_in = din("iota_in", [128, 2])
    wa_in = din("wa_in", [1024, 1536]); wa_out = din("wa_out", [1024, 1024]); gqk = din("gqk", [2, 128])
    ck = din("ck", [2, 2, 256, 128]); cv_ = din("cv_", [2, 2, 256, 128]); rope = din("rope", [1024, 2, 64])
    yout = dout("yout", [NT * 128, D])
    nf = dout("nf", [32, 4, 128, 256]); nb = dout("nb", [32, 4, 128, 256])
    nk = dout("nk", [32, 2, 256, 128]); nv = dout("nv", [32, 2, 256, 128])
    MOD = k.dram([3, 12288], F32)
    X1 = k.dram([NT * 128, D], F32)
    X2 = k.dram([NT * 128, D], F32)
    H2 = k.dram([NT * 128, D], BF16)
    XE = k.dram([20480, D], BF16)
    YE = k.dram([20480, D], F32)

    identb = k.T([128, 128], BF16, "identb")
    identf = k.T([128, 128], F32, "identf")
    cm = k.T([64, 6, 64], F32, "cm")
    negcol = k.T([64, 1], F32, "negcol")
    lnst = k.T([128, 2, 6], F32); lnmv = k.T([128, 2], F32)
    affA = k.T([128, NT, 16], F32, "affA")
    k.dma(identf.ap, identin.ap, [identin], [identf])
    k.V(lambda e: e.tensor_copy(out=identb.ap, in_=identf.ap), [identf], [identb])
    k.dma(cm.ap, cmask.ap, [cmask], [cm])
    k.V(lambda e: e.memset(negcol.ap, -1.0 / 16.0), [], [negcol])
    PERSIST_TOP = k.top

    psi = {}
    pspool = [list(range(8))]

    def nps():
        pool = pspool[0]
        key = tuple(pool)
        psi[key] = (psi.get(key, -1) + 1) % len(pool)
        return PS[pool[psi[key]]]

    def bcast_row(dst, src_dram, row, c0=0):
        n = dst.ap.shape[1]
        k.dma(dst.ap, src_dram.ap[row:row + 1, c0:c0 + n].to_broadcast([128, n]), [src_dram], [dst])

    def get_mod(dst, layer, j, row, plus1=False):
        bcast_row(dst, MOD, row, layer * 6144 + j * 1024)
        if plus1:
            k.V(lambda e: e.tensor_scalar_add(out=dst.ap, in0=dst.ap, scalar1=1.0), [dst], [dst])

    def layernorm(xt, gt, bt):
        st = lnst; mv = lnmv
        k.V(lambda e: e.bn_stats(out=st.ap[:, 0, :], in_=xt.ap[:, 0:512]), [xt], [st])
        k.V(lambda e: e.bn_stats(out=st.ap[:, 1, :], in_=xt.ap[:, 512:1024]), [xt, st], [st])
        k.V(lambda e: e.bn_aggr(out=mv.ap, in_=st.ap.rearrange("p a b -> p (a b)")), [st], [mv])
        k.V(lambda e: e.tensor_scalar_add(out=mv.ap[:, 1:2], in0=mv.ap[:, 1:2], scalar1=EPS), [mv], [mv])
        k.A(lambda e: e.activation(out=mv.ap[:, 1:2], in_=mv.ap[:, 1:2], func=AF.Sqrt), [mv], [mv])
        k.V(lambda e: e.reciprocal(out=mv.ap[:, 1:2], in_=mv.ap[:, 1:2]), [mv], [mv])
        k.V(lambda e: e.tensor_scalar(out=xt.ap, in0=xt.ap, scalar1=mv.ap[:, 0:1], scalar2=mv.ap[:, 1:2],
                                      op0=ALU.subtract, op1=ALU.mult), [xt, mv], [xt])
        k.V(lambda e: e.tensor_mul(out=xt.ap, in0=xt.ap, in1=gt.ap), [xt, gt], [xt])
        k.V(lambda e: e.tensor_add(out=xt.ap, in0=xt.ap, in1=bt.ap), [xt, bt], [xt])

    def load_w_bf16(dst, src_ap, src_buf, kchunks, cols, stage, dcol=0):
        i = 0
        cw = min(cols, 1024)
        step = max(1, 1024 // cw)
        for c0 in range(0, cols, cw):
            cwi = min(cw, cols - c0)
            for k0 in range(0, kchunks, step):
                k1 = min(kchunks, k0 + step)
                st = stage[i % len(stage)]
                i += 1
                v = st.ap[:, 0:(k1 - k0) * cwi].rearrange("p (a b) -> p a b", b=cwi)
                k.dma(v, src_ap[k0 * 128:k1 * 128, c0:c0 + cwi].rearrange("(a p) c -> p a c", p=128), [src_buf], [st])
                k.G(lambda e, v=v, k0=k0, k1=k1, c0=c0, cwi=cwi: e.tensor_copy(out=dst.ap[:, k0:k1, dcol + c0:dcol + c0 + cwi], in_=v), [st], [dst])

    cvt = k.T([128, 8, 3], F32)
    wst = [k.T([128, 8, 512], F32, f"wst{i}") for i in range(2)]
    bm = k.T([3, 512], F32); mo = k.T([3, 512], F32)
    k.dma(cvt.ap, cvT.ap, [cvT], [cvt])
    k.A(lambda e: e.activation(out=cvt.ap, in_=cvt.ap, func=AF.Silu), [cvt], [cvt])
    for i in range(2):
        for cb in range(12):
            w_ = wst[(i * 12 + cb) % 2]
            k.dma(w_.ap, w_mod.ap[i, :, cb * 512:(cb + 1) * 512].rearrange("(k p) c -> p k c", p=128), [w_mod], [w_])
            k.dma(bm.ap, b_mod.ap[i:i + 1, cb * 512:(cb + 1) * 512].to_broadcast([3, 512]), [b_mod], [bm])
            p = nps()
            for kk in range(8):
                k.mm(p.ap[0:3, :], cvt.ap[:, kk, :], w_.ap[:, kk, :], kk == 0, kk == 7, [cvt, w_], [p])
            k.V(lambda e, p=p: e.tensor_add(out=mo.ap, in0=p.ap[0:3, :], in1=bm.ap), [p, bm], [mo])
            k.dma(MOD.ap[:, i * 6144 + cb * 512:i * 6144 + (cb + 1) * 512], mo.ap, [mo], [MOD])
    k.barrier()
    k.top = PERSIST_TOP
    if KSTOP == "M":
        xt = k.T([128, D], F32)
        for j in range(12):
            k.dma(xt.ap, MOD.ap[0:1, j * 1024:(j + 1) * 1024].to_broadcast([128, 1024]), [MOD], [xt])
            k.dma(yout.ap[j * 128:(j + 1) * 128, :], xt.ap, [xt], [yout], is_out=True)
        return k.finish()

    def row_of_tile(t):
        return 0 if t < NTP else (1 if t < NTP + 8 else 2)

    def post_mixer(t, xt, yps, gaT, lgT, lbT, sc2T, sh2T, wrT, tmpT, h2bT, h2TT):
        for half in range(2):
            k.V(lambda e, half=half: e.tensor_mul(out=tmpT.ap[:, half * 512:(half + 1) * 512], in0=yps[half].ap,
                                                 in1=gaT.ap[:, half * 512:(half + 1) * 512]), [yps[half], gaT], [tmpT])
        k.V(lambda e: e.scalar_tensor_tensor(out=xt.ap, in0=xt.ap, scalar=ALPHA, in1=tmpT.ap, op0=ALU.mult, op1=ALU.add), [xt, tmpT], [xt])
        layernorm(xt, lgT, lbT)
        k.dma(X1.ap[t * 128:(t + 1) * 128, :], xt.ap, [xt], [X1])
        k.V(lambda e: e.tensor_mul(out=tmpT.ap, in0=xt.ap, in1=sc2T.ap), [xt, sc2T], [tmpT])
        k.V(lambda e: e.tensor_add(out=tmpT.ap, in0=tmpT.ap, in1=sh2T.ap), [tmpT, sh2T], [tmpT])
        k.A(lambda e: e.copy(out=h2bT.ap, in_=tmpT.ap), [tmpT], [h2bT])
        k.dma(H2.ap[t * 128:(t + 1) * 128, :], h2bT.ap, [h2bT], [H2])
        for half in range(2):
            p = nps()
            for kk in range(4):
                kc = half * 4 + kk
                k.tr(p.ap[:, kk * 128:(kk + 1) * 128], tmpT.ap[:, kc * 128:(kc + 1) * 128], identf.ap, [tmpT, identf], [p])
            k.A(lambda e, p=p, half=half: e.copy(out=h2TT.ap[:, half * 4:half * 4 + 4, :], in_=p.ap.rearrange("p (a b) -> p a b", b=128)), [p], [h2TT])
        pl = nps()
        for kk in range(8):
            k.mm(pl.ap[:, 0:16], h2TT.ap[:, kk, :], wrT.ap[:, kk, :], kk == 0, kk == 7, [h2TT, wrT], [pl])
        mx = lnmv
        k.V(lambda e, pl=pl: e.tensor_reduce(out=mx.ap[:, 0:1], in_=pl.ap[:, 0:16], axis=AX.X, op=ALU.max), [pl], [mx])
        k.V(lambda e, pl=pl: e.tensor_scalar(out=affA.ap[:, t, :], in0=pl.ap[:, 0:16], scalar1=mx.ap[:, 0:1], scalar2=None, op0=ALU.subtract), [pl, mx], [affA])
        k.A(lambda e: e.activation(out=affA.ap[:, t, :], in_=affA.ap[:, t, :], func=AF.Exp), [affA], [affA])
        k.V(lambda e: e.tensor_reduce(out=mx.ap[:, 1:2], in_=affA.ap[:, t, :], axis=AX.X, op=ALU.add), [affA], [mx])
        k.V(lambda e: e.reciprocal(out=mx.ap[:, 1:2], in_=mx.ap[:, 1:2]), [mx], [mx])
        k.V(lambda e: e.tensor_scalar(out=affA.ap[:, t, :], in0=affA.ap[:, t, :], scalar1=mx.ap[:, 1:2], scalar2=None, op0=ALU.mult), [affA, mx], [affA])

    def gla_layer():
        nonlocal_top = PERSIST_TOP
        k.top = PERSIST_TOP
        stage = [k.T([128, 1024], F32, f"stg{i}") for i in range(2)]
        Wo = k.T([128, 8, 1024], BF16, "Wo")
        Wg1 = k.T([128, 8, 32], BF16, "Wg1")
        W2f = k.T([17, 512], BF16); W2b = k.T([17, 512], BF16)
        gnt = k.T([128, 1024], F32)
        scT = k.T([128, D]); shT = k.T([128, D]); tmpT = k.T([128, D])
        wrT = k.T([128, 8, 16], F32)
        h2bT = k.T([128, D], BF16); h2TT = k.T([128, 8, 128], F32)
        xts = [k.T([128, D], F32, "xt0")] * 2
        g1aug = [k.T([17, 64], BF16, f"g1aug{i}") for i in range(2)]
        kgT = k.T([128, 64], BF16, "kgT")
        hb = k.T([128, D], BF16, "hb")
        load_w_bf16(Wo, w_out.ap, w_out, 8, 1024, stage)
        load_w_bf16(Wg1, w_g1.ap, w_g1, 8, 32, stage)
        for dst, src in ((W2f, w_gf2a), (W2b, w_gb2a)):
            k.dma(stage[0].ap[0:17, 0:512], src.ap, [src], [stage[0]])
            k.V(lambda e, dst=dst: e.tensor_copy(out=dst.ap, in_=stage[0].ap[0:17, 0:512]), [stage[0]], [dst])
        bcast_row(gnt, gnorm, 0)
        k.dma(wrT.ap, w_router.ap[0].rearrange("(k p) c -> p k c", p=128), [w_router], [wrT])
        for t_ in g1aug:
            k.V(lambda e, t_=t_: e.memset(t_.ap, 1.0), [], [t_])
        BASE_TOP = k.top

        def alloc_post(row):
            P = dict(ga=k.T([128, D]), sc2=k.T([128, D]), sh2=k.T([128, D]), lg=k.T([128, D]), lb=k.T([128, D]))
            get_mod(P["ga"], 0, 2, row); get_mod(P["sc2"], 0, 4, row, plus1=True); get_mod(P["sh2"], 0, 3, row)
            bcast_row(P["lg"], lng, 0); bcast_row(P["lb"], lnb, 0)
            return P

        def run_seqs(seqs, Wbuf, per_head_load, P):
            T_max = max(nt for _, nt, _, _, _ in seqs) * 128
            hT = k.T([128, 8, T_max], BF16, "hT")
            ogT = k.T([128, 8, T_max], BF16, "ogT")
            SCR_TOP = k.top
            scr = gla_scratch(T_max)
            SEQ_TOP = k.top
            cur_row = [None]
            for (t0, ntile, row, S0, so) in seqs:
                T = ntile * 128
                if cur_row[0] != row:
                    cur_row[0] = row
                    get_mod(scT, 0, 1, row, plus1=True); get_mod(shT, 0, 0, row)
                for ti in range(ntile):
                    xt = xts[ti % 2]
                    k.dma(xt.ap, xin.ap[(t0 + ti) * 128:(t0 + ti + 1) * 128, :], [xin], [xt])
                    k.V(lambda e, xt=xt: e.tensor_mul(out=tmpT.ap, in0=xt.ap, in1=scT.ap), [xt, scT], [tmpT])
                    k.V(lambda e: e.tensor_add(out=hb.ap, in0=tmpT.ap, in1=shT.ap), [tmpT, shT], [hb])
                    for half in range(2):
                        if "hT" in KSKIP:
                            continue
                        p = nps(); pv = p.ap.bitcast(BF16)
                        for kk in range(4):
                            kc = half * 4 + kk
                            k.tr(pv[:, kk * 128:(kk + 1) * 128], hb.ap[:, kc * 128:(kc + 1) * 128], identb.ap, [hb, identb], [p])
                        k.A(lambda e, pv=pv, half=half, ti=ti: e.copy(out=hT.ap[:, half * 4:half * 4 + 4, ti * 128:(ti + 1) * 128],
                                                                      in_=pv[:, 0:512].rearrange("p (a b) -> p a b", b=128)), [p], [hT])
                for h in range(4):
                    if "heads" in KSKIP:
                        continue
                    if per_head_load:
                        load_w_bf16(Wbuf, w_in_hd.ap[:, h * 768:(h + 1) * 768], w_in_hd, 8, 768, stage)
                        wb = 0
                    else:
                        wb = h * 768

                    def cb(c, oh, h=h):
                        p = nps(); pv = p.ap.bitcast(BF16)
                        for kk in range(2):
                            k.tr(pv[:, kk * 64:(kk + 1) * 64], oh.ap[:, kk * 128:(kk + 1) * 128], identb.ap[0:64, 0:64], [oh, identb], [p])
                        k.A(lambda e, pv=pv: e.copy(out=ogT.ap[:, 2 * h:2 * h + 2, c * 64:(c + 1) * 64],
                                                    in_=pv[:, 0:128].rearrange("p (a b) -> p a b", b=64)), [p], [ogT])
                    S0h = None if S0 is None else (S0[0].ap[S0[2], h], S0[0], S0[1].ap[S0[2], h], S0[1])
                    Sf, Sb = gla_head(scr, hT, T, Wbuf, wb, W2f, W2b, h * 128, gnt.ap[0:64, h * 256:(h + 1) * 256], gnt, S0h, cb,
                                      Wg1, g1aug, kgT)
                    if so is not None:
                        k.dma(nf.ap[so, h], Sf.ap, [Sf], [nf], is_out=True)
                        k.dma(nb.ap[so, h], Sb.ap, [Sb], [nb], is_out=True)
                if P is None:
                    k.barrier()
                    k.top = SCR_TOP
                    PP = alloc_post(row)
                else:
                    PP = P
                for ti in range(ntile):
                    if "post" in KSKIP:
                        continue
                    xt = xts[ti % 2]
                    k.dma(xt.ap, xin.ap[(t0 + ti) * 128:(t0 + ti + 1) * 128, :], [xin], [xt])
                    yps = []
                    for half in range(2):
                        py = nps()
                        for kk in range(8):
                            k.mm(py.ap, ogT.ap[:, kk, ti * 128:(ti + 1) * 128], Wo.ap[:, kk, half * 512:(half + 1) * 512], kk == 0, kk == 7, [ogT, Wo], [py])
                        yps.append(py)
                    post_mixer(t0 + ti, xt, yps, PP["ga"], PP["lg"], PP["lb"], PP["sc2"], PP["sh2"], wrT, tmpT, h2bT, h2TT)
                if P is None:
                    k.barrier()

        k.top = BASE_TOP
        Wp4 = k.T([128, 8, 3072], BF16, "Wp4")
        load_w_bf16(Wp4, w_in_hd.ap, w_in_hd, 8, 3072, stage)
        nseq = 32 if KSTOP not in ("G1", "G0") else 1
        run_seqs([(2 * s, 2, 0, None, s) for s in range(nseq)], Wp4, False, None)
        k.barrier()
        k.top = BASE_TOP
        Whd = k.T([128, 8, 768], BF16, "Whd")
        if KSTOP != "G0":
            run_seqs([(NTP + 8 * b, 8, 1 + b, (S0f, S0b, b), None) for b in range(2)], Whd, True, None)
        k.barrier()
        k.top = PERSIST_TOP

    def gla_scratch(T):
        class S_: pass
        S = S_()
        nch = T // 64
        S.vS = k.T([64, nch, 256], BF16, "vS")
        S.srS = k.T([64, nch, 256], F32, "srS")
        S.qgT = k.T([128, nch, 2, 64], BF16, "qgT")
        S.AT = k.T([64, nch, 2, 64], BF16, "AT")
        S.Sin = k.T([128, nch, 2, 256], BF16, "Sin")
        S.kdb = k.T([64, nch, 128], BF16, "kdb")
        S.decb = k.T([128, nch], F32, "decb")
        S.Sf = k.T([128, 256], F32, "Sf"); S.Sb = k.T([128, 256], F32, "Sb")
        S.decf = [k.T([128, 1], F32, f"decf{i}") for i in range(2)]
        S.qs = [k.T([64, 128], F32, f"qs{i}") for i in range(2)]; S.ks = [k.T([64, 128], F32, f"ks{i}") for i in range(2)]
        S.lf = [k.T([64, 2, 128], F32, f"lf{i}") for i in range(2)]
        S.ex = [k.T([64, 3, 128], F32, f"ex{i}") for i in range(2)]
        S.qg = [k.T([64, 128], BF16, f"qg{i}") for i in range(2)]; S.kg = [k.T([64, 128], BF16, f"kg{i}") for i in range(2)]; S.kd = k.T([64, 128], BF16, "kd")
        S.sq = [k.T([64, 256], F32, f"sq{i}") for i in range(2)] if T <= 256 else [k.T([64, 256], F32, "sq0")] * 2; S.of = [k.T([64, 256], F32, f"of{i}") for i in range(2)] if T <= 256 else [k.T([64, 256], F32, "of0")] * 2; S.ss = [k.T([64, 1], F32, f"ss{i}") for i in range(2)]
        S.kgT = [k.T([128, 64], BF16, f"kgT{i}") for i in range(2)]
        S.ogh = [k.T([64, 256], BF16, f"ogh{i}") for i in range(2)]
        return S

    def gla_head(scr, hT, T, Wb, wb, W2f, W2b, w2c, gn_ap, gn_buf, S0, out_cb, Wg1, g1aug, kgT):
        nch = T // 64
        vS = scr.vS
        srS = scr.srS
        qgT = scr.qgT
        AT = scr.AT
        Sin = scr.Sin
        kdb = scr.kdb
        decb = scr.decb
        Sf = scr.Sf
        Sb = scr.Sb
        kd = scr.kd
        ogh = scr.ogh
        if S0 is None:
            k.V(lambda e: e.memset(Sf.ap, 0.0), [], [Sf])
            k.V(lambda e: e.memset(Sb.ap, 0.0), [], [Sb])
        else:
            k.dma(Sf.ap, S0[0], [S0[1]], [Sf])
            k.dma(Sb.ap, S0[2], [S0[3]], [Sb])
        def p1a(c):
                if KCUT == -1:
                    return
                qs = scr.qs[c % 2]; ks = scr.ks[c % 2]; lf = scr.lf[c % 2]; decf = scr.decf[c % 2]
                tk = slice(c * 64, (c + 1) * 64)
                pq = nps()
                for kk in range(8):
                    k.mm(pq.ap[0:64, 0:256], hT.ap[:, kk, tk], Wb.ap[:, kk, wb:wb + 256], kk == 0, kk == 7, [hT, Wb], [pq])
                k.A(lambda e, pq=pq: e.copy(out=qs.ap, in_=pq.ap[0:64, 0:128]), [pq], [qs])
                k.A(lambda e, pq=pq: e.copy(out=ks.ap, in_=pq.ap[0:64, 128:256]), [pq], [ks])
                if KCUT == -2:
                    return
                pv_ = nps()
                for kk in range(8):
                    k.mm(pv_.ap[0:64, 0:512], hT.ap[:, kk, tk], Wb.ap[:, kk, wb + 256:wb + 768], kk == 0, kk == 7, [hT, Wb], [pv_])
                k.V(lambda e, pv_=pv_, c=c: e.tensor_copy(out=vS.ap[:, c, :], in_=pv_.ap[0:64, 0:256]), [pv_], [vS])
                k.A(lambda e, pv_=pv_, c=c: e.activation(out=srS.ap[:, c, :], in_=pv_.ap[0:64, 256:512], func=AF.Exp, scale=-1.0), [pv_], [srS])
                k.V(lambda e, c=c: e.tensor_scalar_add(out=srS.ap[:, c, :], in0=srS.ap[:, c, :], scalar1=1.0), [srS], [srS])
                k.V(lambda e, c=c: e.reciprocal(out=srS.ap[:, c, :], in_=srS.ap[:, c, :]), [srS], [srS])
                k.V(lambda e, pv_=pv_, c=c: e.tensor_mul(out=srS.ap[:, c, :], in0=srS.ap[:, c, :], in1=pv_.ap[0:64, 256:512]), [srS, pv_], [srS])
                if KCUT == 1:
                    return
                for d_, (W2_, ga) in enumerate(((W2f, g1aug[0]), (W2b, g1aug[1]))):
                    pg = nps()
                    for kk in range(8):
                        k.mm(pg.ap[0:16, 0:64], Wg1.ap[:, kk, d_ * 16:(d_ + 1) * 16], hT.ap[:, kk, tk], kk == 0, kk == 7, [Wg1, hT], [pg])
                    k.V(lambda e, pg=pg, ga=ga: e.tensor_copy(out=ga.ap[0:16, :], in_=pg.ap[0:16, 0:64]), [pg], [ga])
                    pz = nps()
                    k.mm(pz.ap[0:64, 0:128], ga.ap, W2_.ap[:, w2c:w2c + 128], True, True, [ga, W2_], [pz])
                    k.A(lambda e, pz=pz, d_=d_: e.activation(out=lf.ap[:, d_, :], in_=pz.ap[0:64, 0:128], func=AF.Exp, scale=-1.0), [pz], [lf])
                    k.V(lambda e, d_=d_: e.tensor_scalar_add(out=lf.ap[:, d_, :], in0=lf.ap[:, d_, :], scalar1=1.0), [lf], [lf])
                    k.A(lambda e, d_=d_: e.activation(out=lf.ap[:, d_, :], in_=lf.ap[:, d_, :], func=AF.Ln), [lf], [lf])
                if KCUT == 2:
                    return
        def p1d(c, d_):
                qs = scr.qs[c % 2]; ks = scr.ks[c % 2]; lf = scr.lf[c % 2]; decf = scr.decf[c % 2]
                ex = scr.ex[d_]; qg = scr.qg[d_]; kg = scr.kg[d_]; kgT = scr.kgT[d_]
                pG = nps()
                k.mm(pG.ap[0:64, 0:128], cm.ap[:, 2 * d_, :], lf.ap[:, d_, :], True, True, [cm, lf], [pG])
                k.mm(pG.ap[0:64, 128:256], cm.ap[:, 2 * d_ + 1, :], lf.ap[:, d_, :], True, True, [cm, lf], [pG])
                pdc = nps()
                k.mm(pdc.ap[:, 0:1], lf.ap[:, d_, :], negcol.ap, True, True, [lf, negcol], [pdc])
                if d_ == 0:
                    k.A(lambda e, pdc=pdc: e.activation(out=decf.ap, in_=pdc.ap[:, 0:1], func=AF.Exp), [pdc], [decf])
                else:
                    k.A(lambda e, pdc=pdc, c=c: e.activation(out=decb.ap[:, c:c + 1], in_=pdc.ap[:, 0:1], func=AF.Exp), [pdc], [decb])
                k.A(lambda e, pG=pG: e.activation(out=ex.ap[:, 0, :], in_=pG.ap[0:64, 0:128], func=AF.Exp), [pG], [ex])
                k.A(lambda e, pG=pG: e.activation(out=ex.ap[:, 1, :], in_=pG.ap[0:64, 0:128], func=AF.Exp, scale=-1.0), [pG], [ex])
                k.A(lambda e, pG=pG: e.activation(out=ex.ap[:, 2, :], in_=pG.ap[0:64, 128:256], func=AF.Exp), [pG], [ex])
                if KCUT == 3:
                    return
                k.V(lambda e: e.scalar_tensor_tensor(out=qg.ap, in0=qs.ap, scalar=128.0 ** -0.5, in1=ex.ap[:, 0, :],
                                                     op0=ALU.mult, op1=ALU.mult), [qs, ex], [qg])
                k.V(lambda e: e.tensor_mul(out=kg.ap, in0=ks.ap, in1=ex.ap[:, 1, :]), [ks, ex], [kg])
                kdd = kd if d_ == 0 else kdb
                kd_ap = kd.ap if d_ == 0 else kdb.ap[:, c, :]
                k.V(lambda e, kd_ap=kd_ap: e.tensor_mul(out=kd_ap, in0=ks.ap, in1=ex.ap[:, 2, :]), [ks, ex], [kdd])
                pT = nps(); pTv = pT.ap.bitcast(BF16)
                k.tr(pTv[:, 0:64], qg.ap, identb.ap[0:64, 0:64], [qg, identb], [pT])
                k.tr(pTv[:, 64:128], kg.ap, identb.ap[0:64, 0:64], [kg, identb], [pT])
                k.A(lambda e, pTv=pTv, c=c, d_=d_: e.copy(out=qgT.ap[:, c, d_, :], in_=pTv[:, 0:64]), [pT], [qgT])
                k.V(lambda e, pTv=pTv: e.tensor_copy(out=kgT.ap, in_=pTv[:, 64:128]), [pT], [kgT])
                if KCUT == 4:
                    return
                pA = nps()
                k.mm(pA.ap[0:64, 0:64], kgT.ap, qgT.ap[:, c, d_, :], True, True, [kgT, qgT], [pA])
                k.V(lambda e, pA=pA, c=c, d_=d_: e.tensor_mul(out=AT.ap[:, c, d_, :], in0=pA.ap[0:64, 0:64], in1=cm.ap[:, 4 + d_, :]), [pA, cm], [AT])
                if d_ == 0:
                    pS = nps()
                    k.mm(pS.ap[:, 0:256], kd.ap, vS.ap[:, c, :], True, True, [kd, vS], [pS])
                    k.A(lambda e, c=c: e.copy(out=Sin.ap[:, c, 0, :], in_=Sf.ap), [Sf], [Sin])
                    k.V(lambda e, pS=pS: e.scalar_tensor_tensor(out=Sf.ap, in0=Sf.ap, scalar=decf.ap[:, 0:1], in1=pS.ap[:, 0:256],
                                                                 op0=ALU.mult, op1=ALU.add), [Sf, decf, pS], [Sf])

        def with_pool(pool, fn, *args):
            def th():
                old = pspool[0]
                pspool[0] = pool
                try:
                    fn(*args)
                finally:
                    pspool[0] = old
            return th
        if KCUT != 0:
            for c in range(nch):
                p1a(c)
                for d_ in range(2):
                    p1d(c, d_)
        else:
            k.interleave([with_pool([6, 7], p1a, 0)])
            for c in range(nch):
                ths = [with_pool([0, 1, 2], p1d, c, 0), with_pool([3, 4, 5], p1d, c, 1)]
                if c + 1 < nch:
                    ths.append(with_pool([6, 7], p1a, c + 1))
                k.interleave(ths)

        if KCUT != 0 and KCUT <= 5:
            return Sf, Sb
        for c in range(nch - 1, -1, -1):
            pS = nps()
            k.mm(pS.ap[:, 0:256], kdb.ap[:, c, :], vS.ap[:, c, :], True, True, [kdb, vS], [pS])
            k.A(lambda e, c=c: e.copy(out=Sin.ap[:, c, 1, :], in_=Sb.ap), [Sb], [Sin])
            k.V(lambda e, pS=pS, c=c: e.scalar_tensor_tensor(out=Sb.ap, in0=Sb.ap, scalar=decb.ap[:, c:c + 1], in1=pS.ap[:, 0:256],
                                                              op0=ALU.mult, op1=ALU.add), [Sb, decb, pS], [Sb])
        if KCUT == 6:
            return Sf, Sb
        def p2(c):
                sq = scr.sq[c % 2]; of = scr.of[c % 2]; ss = scr.ss[c % 2]
                po = nps()
                o_ap = po.ap[0:64, 0:256]
                k.mm(o_ap, AT.ap[:, c, 0, :], vS.ap[:, c, :], True, False, [AT, vS], [po])
                k.mm(o_ap, AT.ap[:, c, 1, :], vS.ap[:, c, :], False, False, [AT, vS], [po])
                k.mm(o_ap, qgT.ap[:, c, 0, :], Sin.ap[:, c, 0, :], False, False, [qgT, Sin], [po])
                k.mm(o_ap, qgT.ap[:, c, 1, :], Sin.ap[:, c, 1, :], False, True, [qgT, Sin], [po])
                k.A(lambda e, po=po: e.copy(out=of.ap, in_=po.ap[0:64, 0:256]), [po], [of])
                if KCUT == 7:
                    return
                k.V(lambda e: e.tensor_mul(out=sq.ap, in0=of.ap, in1=of.ap), [of], [sq])
                k.V(lambda e: e.tensor_reduce(out=ss.ap, in_=sq.ap, axis=AX.X, op=ALU.add), [sq], [ss])
                k.V(lambda e: e.tensor_scalar(out=ss.ap, in0=ss.ap, scalar1=1.0 / 256.0, scalar2=EPS, op0=ALU.mult, op1=ALU.add), [ss], [ss])
                k.A(lambda e: e.activation(out=ss.ap, in_=ss.ap, func=AF.Ln), [ss], [ss])
                k.A(lambda e: e.activation(out=ss.ap, in_=ss.ap, func=AF.Exp, scale=-0.5), [ss], [ss])
                k.V(lambda e: e.scalar_tensor_tensor(out=of.ap, in0=of.ap, scalar=ss.ap[:, 0:1], in1=gn_ap, op0=ALU.mult, op1=ALU.mult), [of, ss, gn_buf], [of])
                oh = ogh[c % 2]
                k.V(lambda e, c=c, oh=oh: e.tensor_mul(out=oh.ap, in0=of.ap, in1=srS.ap[:, c, :]), [of, srS], [oh])
                if KCUT == 8:
                    return
                out_cb(c, oh)
        if T <= 256:
            for c in range(0, nch, 2):
                k.interleave([with_pool([0, 1, 2, 3], p2, c), with_pool([4, 5, 6, 7], p2, c + 1)])
        else:
            for c in range(nch):
                p2(c)
        return Sf, Sb

    if KSTOP in ("A", "A1"):
        xt_ = k.T([128, D], F32)
        for t in range(NT):
            k.dma(xt_.ap, xin.ap[t * 128:(t + 1) * 128, :], [xin], [xt_])
            k.dma(X2.ap[t * 128:(t + 1) * 128, :], xt_.ap, [xt_], [X2])
        k.barrier()
        k.top = PERSIST_TOP
    else:
        gla_layer()
    if KSTOP in ("G", "G1", "G0"):
        xt = k.T([128, D], F32)
        tl = list(range(NT)) if KSTOP == "G" else ([0, 1] if KSTOP == "G0" else [0, 1] + list(range(64, 80)))
        for t in tl:
            k.dma(xt.ap, X1.ap[t * 128:(t + 1) * 128, :], [X1], [xt])
            k.dma(yout.ap[t * 128:(t + 1) * 128, :], xt.ap, [xt], [yout], is_out=True)
        return k.finish()
    ustrict = din_late("ustrict", [128, 128])
    basein = din_late("basein", [128, 32])

    def moe_layer(layer, out_dram):
        k.barrier()
        k.top = PERSIST_TOP
        GRP = ((0, NTP, 1024.0), (NTP, NT, 256.0))
        us = k.T([128, 128], F32, "us"); ones = k.T([128, 128], F32, "ones")
        k.dma(us.ap, ustrict.ap, [ustrict], [us])
        k.V(lambda e: e.memset(ones.ap, 1.0), [], [ones])
        iot = k.T([128, 2], F32, "iot")
        k.dma(iot.ap, iota_in.ap, [iota_in], [iot])
        baseT = k.T([128, 2, 16], F32, "baseT")
        k.dma(baseT.ap, basein.ap.rearrange("p (g e) -> p g e", e=16), [basein], [baseT])
        idxI = k.T([128, NT, 16], I32, "idxI")
        ROUTE_TOP = k.top
        lo = k.T([128, 2, 16], F32, "lo"); hi = k.T([128, 2, 16], F32, "hi"); mid = k.T([128, 2, 16], F32, "mid")
        Cv = k.T([128, 2, 16], F32, "Cv"); cntp = k.T([128, 2, 16], F32, "cntp")
        gem = k.T([128, 2, 16], U32, "gem"); ltm = k.T([128, 2, 16], U32, "ltm")
        cmp = k.T([128, NT, 16], F32, "cmp")
        k.V(lambda e: e.memset(lo.ap, 0.0), [], [lo])
        k.V(lambda e: e.memset(hi.ap, 1.0), [], [hi])
        for g, (t0, t1, C) in enumerate(GRP):
            k.V(lambda e, g=g, C=C: e.memset(Cv.ap[:, g, :], C), [], [Cv])
        NIT = int(os.environ.get("KNIT", "30"))
        for it in range(NIT):
            k.V(lambda e: e.tensor_add(out=mid.ap, in0=lo.ap, in1=hi.ap), [lo, hi], [mid])
            k.V(lambda e: e.tensor_scalar(out=mid.ap, in0=mid.ap, scalar1=0.5, scalar2=None, op0=ALU.mult), [mid], [mid])
            for g, (t0, t1, C) in enumerate(GRP):
                nt = t1 - t0
                k.V(lambda e, g=g, t0=t0, t1=t1, nt=nt: e.tensor_tensor(out=cmp.ap[:, t0:t1, :], in0=affA.ap[:, t0:t1, :],
                                                                        in1=mid.ap[:, g, :].unsqueeze(1).to_broadcast([128, nt, 16]), op=ALU.is_ge), [affA, mid], [cmp])
                k.V(lambda e, g=g, t0=t0, t1=t1: e.tensor_reduce(out=cntp.ap[:, g, :], in_=cmp.ap[:, t0:t1, :].rearrange("p t e -> p e t"), axis=AX.X, op=ALU.add), [cmp], [cntp])
            pt = nps()
            k.mm(pt.ap[:, 0:32], ones.ap, cntp.ap.rearrange("p g e -> p (g e)"), True, True, [ones, cntp], [pt])
            k.V(lambda e, pt=pt: e.tensor_tensor(out=gem.ap.rearrange("p g e -> p (g e)"), in0=pt.ap[:, 0:32], in1=Cv.ap.rearrange("p g e -> p (g e)"), op=ALU.is_ge), [pt, Cv], [gem])
            k.V(lambda e, pt=pt: e.tensor_tensor(out=ltm.ap.rearrange("p g e -> p (g e)"), in0=pt.ap[:, 0:32], in1=Cv.ap.rearrange("p g e -> p (g e)"), op=ALU.is_lt), [pt, Cv], [ltm])
            k.V(lambda e: e.copy_predicated(out=lo.ap, mask=gem.ap, data=mid.ap), [gem, mid, lo], [lo])
            k.V(lambda e: e.copy_predicated(out=hi.ap, mask=ltm.ap, data=mid.ap), [ltm, mid, hi], [hi])
        mask = cmp
        within = k.T([128, NT, 16], F32, "within"); cntS = k.T([128, NT, 16], F32, "cntS"); incl = k.T([128, NT, 16], F32, "incl")
        onesr = k.T([128, NT], F32, "onesr")
        k.V(lambda e: e.memset(onesr.ap, 1.0), [], [onesr])
        for g, (t0, t1, C) in enumerate(GRP):
            nt = t1 - t0
            k.V(lambda e, g=g, t0=t0, t1=t1, nt=nt: e.tensor_tensor(out=mask.ap[:, t0:t1, :], in0=affA.ap[:, t0:t1, :],
                                                                    in1=lo.ap[:, g, :].unsqueeze(1).to_broadcast([128, nt, 16]), op=ALU.is_ge), [affA, lo], [mask])
        mflat = mask.ap.rearrange("p t e -> p (t e)")
        for c0 in range(0, NT * 16, 512):
            c1 = min(NT * 16, c0 + 512)
            pw = nps(); pc = nps()
            k.mm(pw.ap[:, 0:c1 - c0], us.ap, mflat[:, c0:c1], True, True, [us, mask], [pw])
            k.mm(pc.ap[:, 0:c1 - c0], ones.ap, mflat[:, c0:c1], True, True, [ones, mask], [pc])
            k.V(lambda e, pw=pw, c0=c0, c1=c1: e.tensor_copy(out=within.ap.rearrange("p t e -> p (t e)")[:, c0:c1], in_=pw.ap[:, 0:c1 - c0]), [pw], [within])
            k.A(lambda e, pc=pc, c0=c0, c1=c1: e.copy(out=cntS.ap.rearrange("p t e -> p (t e)")[:, c0:c1], in_=pc.ap[:, 0:c1 - c0]), [pc], [cntS])
        for g, (t0, t1, C) in enumerate(GRP):
            for e_ in range(16):
                k.V(lambda e, t0=t0, t1=t1, e_=e_: e.tensor_tensor_scan(out=incl.ap[:, t0:t1, e_], data0=onesr.ap[:, t0:t1], data1=cntS.ap[:, t0:t1, e_],
                                                                        initial=0.0, op0=ALU.mult, op1=ALU.add), [onesr, cntS], [incl])
        pos = within
        k.V(lambda e: e.tensor_add(out=pos.ap, in0=within.ap, in1=incl.ap), [within, incl], [pos])
        k.V(lambda e: e.tensor_sub(out=pos.ap, in0=pos.ap, in1=cntS.ap), [pos, cntS], [pos])
        sel = incl
        for g, (t0, t1, C) in enumerate(GRP):
            nt = t1 - t0
            k.V(lambda e, t0=t0, t1=t1, C=C: e.tensor_scalar(out=sel.ap[:, t0:t1, :], in0=pos.ap[:, t0:t1, :], scalar1=C, scalar2=None, op0=ALU.is_lt), [pos], [sel])
            k.V(lambda e, g=g, t0=t0, t1=t1, nt=nt: e.tensor_add(out=pos.ap[:, t0:t1, :], in0=pos.ap[:, t0:t1, :],
                                                                 in1=baseT.ap[:, g, :].unsqueeze(1).to_broadcast([128, nt, 16])), [pos, baseT], [pos])
        k.V(lambda e: e.tensor_mul(out=sel.ap, in0=sel.ap, in1=mask.ap), [sel, mask], [sel])
        oob = k.T([128, 1], F32, "oob")
        k.V(lambda e: e.tensor_scalar_add(out=oob.ap, in0=iot.ap[:, 0:1], scalar1=30000.0), [iot], [oob])
        k.V(lambda e: e.tensor_scalar(out=pos.ap, in0=pos.ap, scalar1=oob.ap[:, 0:1], scalar2=None, op0=ALU.subtract), [pos, oob], [pos])
        k.V(lambda e: e.tensor_mul(out=pos.ap, in0=pos.ap, in1=sel.ap), [pos, sel], [pos])
        k.V(lambda e: e.tensor_scalar(out=pos.ap, in0=pos.ap, scalar1=oob.ap[:, 0:1], scalar2=None, op0=ALU.add), [pos, oob], [pos])
        k.V(lambda e: e.tensor_copy(out=idxI.ap, in_=pos.ap), [pos], [idxI])
        k.barrier()
        k.top = ROUTE_TOP
        if KSTOP == "R":
            return
        h2t = [k.T([128, D], BF16, f"h2t{i}") for i in range(3)]
        for t in range(NT):
            ht = h2t[t % 3]
            k.dma(ht.ap, H2.ap[t * 128:(t + 1) * 128, :], [H2], [ht])
            for e_ in range(16):
                k.emit("pool", lambda e, ht=ht, t=t, e_=e_: e.indirect_dma_start(
                    out=XE.ap, out_offset=bass.IndirectOffsetOnAxis(ap=idxI.ap[:, t, e_:e_ + 1], axis=0),
                    in_=ht.ap, in_offset=None, bounds_check=k.bc_reg, oob_is_err=False), [ht, idxI], [XE], dma=True)
        k.barrier()
        k.top = ROUTE_TOP
        if KSTOP == "D":
            return
        xet = [k.T([128, D], BF16, f"xet{i}") for i in range(2)]
        xeT = k.T([128, 8, 1280], BF16, "xeT")
        hidT = k.T([128, 16, 1280], BF16, "hidT")
        wst = [k.T([128, 2048], F32, f"wst{i}") for i in range(2)]
        wgb = [k.T([128, 8, 512], BF16, f"wgb{i}") for i in range(2)]
        wub = [k.T([128, 8, 512], BF16, f"wub{i}") for i in range(2)]
        wdb = k.T([128, 16, 1024], BF16, "wdb")
        sg_t = k.T([128, 512], F32, "sg_t")
        yet = [k.T([128, D], F32, f"yet{i}") for i in range(2)]
        nexp = 16 if KSTOP != "F1" else 1
        wi = [0]

        def load_cast(dst_ap, dst_buf, src_ap, src_buf, a, b):
            st = wst[wi[0] % 2]; wi[0] += 1
            v = st.ap[:, 0:a * b].rearrange("p (a b) -> p a b", b=b)
            k.dma(v, src_ap, [src_buf], [st])
            k.G(lambda e: e.tensor_copy(out=dst_ap, in_=v), [st], [dst_buf])

        for e_ in range(nexp):
            for sb in range(10):
                xt_ = xet[sb % 2]
                k.dma(xt_.ap, XE.ap[e_ * 1280 + sb * 128:e_ * 1280 + (sb + 1) * 128, :], [XE], [xt_])
                for half in range(2):
                    p = nps(); pv = p.ap.bitcast(BF16)
                    for kk in range(4):
                        kc = half * 4 + kk
                        k.tr(pv[:, kk * 128:(kk + 1) * 128], xt_.ap[:, kc * 128:(kc + 1) * 128], identb.ap, [xt_, identb], [p])
                    k.A(lambda e, pv=pv, half=half, sb=sb: e.copy(out=xeT.ap[:, half * 4:half * 4 + 4, sb * 128:(sb + 1) * 128],
                                                                  in_=pv[:, 0:512].rearrange("p (a b) -> p a b", b=128)), [p], [xeT])
            for fb in range(4):
                wg_ = wgb[fb % 2]; wu_ = wub[fb % 2]
                for kh in range(2):
                    load_cast(wg_.ap[:, kh * 4:(kh + 1) * 4, :], wg_, wg.ap[layer, e_, kh * 512:(kh + 1) * 512, fb * 512:(fb + 1) * 512].rearrange("(k p) c -> p k c", p=128), wg, 4, 512)
                    load_cast(wu_.ap[:, kh * 4:(kh + 1) * 4, :], wu_, wu.ap[layer, e_, kh * 512:(kh + 1) * 512, fb * 512:(fb + 1) * 512].rearrange("(k p) c -> p k c", p=128), wu, 4, 512)
                for f4 in range(4):
                    fc = fb * 4 + f4
                    for (s0, s1) in ((0, 512), (512, 1024), (1024, 1280)):
                        pg_ = nps(); pu_ = nps()
                        for kk in range(8):
                            k.mm(pg_.ap[:, 0:s1 - s0], wg_.ap[:, kk, f4 * 128:(f4 + 1) * 128], xeT.ap[:, kk, s0:s1], kk == 0, kk == 7, [wg_, xeT], [pg_])
                        for kk in range(8):
                            k.mm(pu_.ap[:, 0:s1 - s0], wu_.ap[:, kk, f4 * 128:(f4 + 1) * 128], xeT.ap[:, kk, s0:s1], kk == 0, kk == 7, [wu_, xeT], [pu_])
                        k.A(lambda e, pg_=pg_, s0=s0, s1=s1: e.activation(out=sg_t.ap[:, 0:s1 - s0], in_=pg_.ap[:, 0:s1 - s0], func=AF.Silu), [pg_], [sg_t])
                        k.V(lambda e, pu_=pu_, s0=s0, s1=s1, fc=fc: e.tensor_mul(out=hidT.ap[:, fc, s0:s1], in0=sg_t.ap[:, 0:s1 - s0], in1=pu_.ap[:, 0:s1 - s0]), [sg_t, pu_], [hidT])
            for fb in range(8):
                load_cast(wdb.ap[:, fb * 2:(fb + 1) * 2, :], wdb, wd.ap[layer, e_, fb * 256:(fb + 1) * 256, :].rearrange("(k p) c -> p k c", p=128), wd, 2, 1024)
            for sb in range(10):
                yt_ = yet[sb % 2]
                for half in range(2):
                    py = nps()
                    for fc in range(16):
                        k.mm(py.ap, hidT.ap[:, fc, sb * 128:(sb + 1) * 128], wdb.ap[:, fc, half * 512:(half + 1) * 512], fc == 0, fc == 15, [hidT, wdb], [py])
                    k.A(lambda e, py=py, half=half, yt_=yt_: e.copy(out=yt_.ap[:, half * 512:(half + 1) * 512], in_=py.ap), [py], [yt_])
                k.dma(YE.ap[e_ * 1280 + sb * 128:e_ * 1280 + (sb + 1) * 128, :], yt_.ap, [yt_], [YE])
        k.barrier()
        k.top = ROUTE_TOP
        if KSTOP in ("F", "F1"):
            return
        Gt = [k.T([128, 16, D], F32, f"Gt{i}") for i in range(2)]
        xts = [k.T([128, D], F32, f"cx{i}") for i in range(2)]
        acc = k.T([128, D], F32, "acc")
        gaT = k.T([128, D]); lgT = k.T([128, D]); lbT = k.T([128, D])
        bcast_row(lgT, lng, layer * 2 + 1); bcast_row(lbT, lnb, layer * 2 + 1)
        cur = None
        for t in range(NT):
            row = row_of_tile(t)
            if row != cur:
                cur = row
                get_mod(gaT, layer, 5, row)
            G_ = Gt[t % 2]; xt = xts[t % 2]
            k.G(lambda e, G_=G_: e.memset(G_.ap, 0.0), [], [G_])
            for e_ in range(16):
                k.emit("pool", lambda e, G_=G_, t=t, e_=e_: e.indirect_dma_start(
                    out=G_.ap[:, e_, :], out_offset=None, in_=YE.ap,
                    in_offset=bass.IndirectOffsetOnAxis(ap=idxI.ap[:, t, e_:e_ + 1], axis=0),
                    bounds_check=k.bc_reg, oob_is_err=False), [YE, idxI], [G_], dma=True)
            k.dma(xt.ap, X1.ap[t * 128:(t + 1) * 128, :], [X1], [xt])
            k.V(lambda e, G_=G_, t=t: e.tensor_scalar(out=acc.ap, in0=G_.ap[:, 0, :], scalar1=affA.ap[:, t, 0:1], scalar2=None, op0=ALU.mult), [G_, affA], [acc])
            for e_ in range(1, 16):
                k.V(lambda e, G_=G_, t=t, e_=e_: e.scalar_tensor_tensor(out=acc.ap, in0=G_.ap[:, e_, :], scalar=affA.ap[:, t, e_:e_ + 1], in1=acc.ap,
                                                                        op0=ALU.mult, op1=ALU.add), [G_, affA, acc], [acc])
            k.V(lambda e: e.tensor_mul(out=acc.ap, in0=acc.ap, in1=gaT.ap), [acc, gaT], [acc])
            k.V(lambda e, xt=xt: e.scalar_tensor_tensor(out=xt.ap, in0=xt.ap, scalar=ALPHA, in1=acc.ap, op0=ALU.mult, op1=ALU.add), [xt, acc], [xt])
            layernorm(xt, lgT, lbT)
            k.dma(out_dram.ap[t * 128:(t + 1) * 128, :], xt.ap, [xt], [out_dram], is_out=(out_dram is yout))
        k.barrier()
        k.top = PERSIST_TOP

    if KSTOP != "MOEONLY":
        pass
    if KSTOP not in ("A", "A1"):
        moe_layer(0, X2 if KSTOP not in ("R", "D", "F", "F1", "C") else yout)
    if KSTOP in ("F", "F1"):
        xt = k.T([128, D], F32)
        for t in range(10):
            k.dma(xt.ap, YE.ap[t * 128:(t + 1) * 128, :], [YE], [xt])
            k.dma(yout.ap[t * 128:(t + 1) * 128, :], xt.ap, [xt], [yout], is_out=True)
    if KSTOP in ("R", "D", "F", "F1", "C"):
        return k.finish()
    def attn_layer():
        k.barrier()
        k.top = PERSIST_TOP
        LY = 1
        stage = [k.T([128, 1024], F32, f"stg{i}") for i in range(2)]
        Wa = k.T([128, 8, 1536], BF16, "Wa"); Wao = k.T([128, 8, 1024], BF16, "Wao")
        gqT = k.T([128, 128], F32); gkT = k.T([128, 128], F32)
        scT = k.T([128, D]); shT = k.T([128, D]); tmpT = k.T([128, D])
        wrT = k.T([128, 8, 16], F32)
        h2bT = k.T([128, D], BF16); h2TT = k.T([128, 8, 128], F32)
        xts = [k.T([128, D], F32, f"xt{i}") for i in range(2)]
        hb = k.T([128, D], BF16, "hb")
        load_w_bf16(Wa, wa_in.ap, wa_in, 8, 1536, stage)
        load_w_bf16(Wao, wa_out.ap, wa_out, 8, 1024, stage)
        bcast_row(gqT, gqk, 0); bcast_row(gkT, gqk, 1)
        k.dma(wrT.ap, w_router.ap[LY].rearrange("(k p) c -> p k c", p=128), [w_router], [wrT])
        BASE_TOP = k.top

        def alloc_post(row):
            P = dict(ga=k.T([128, D]), sc2=k.T([128, D]), sh2=k.T([128, D]), lg=k.T([128, D]), lb=k.T([128, D]))
            get_mod(P["ga"], LY, 2, row); get_mod(P["sc2"], LY, 4, row, plus1=True); get_mod(P["sh2"], LY, 3, row)
            bcast_row(P["lg"], lng, LY * 2); bcast_row(P["lb"], lnb, LY * 2)
            return P

        def run_seqs(seqs, T, nctx):
            ntile = T // 128
            Tk = T + nctx
            nkt = Tk // 128
            CH = min(T, 512)
            hT = k.T([128, 8, T], BF16, "hT")
            oS_ap = hT.ap.rearrange("p a t -> p (a t)").rearrange("p (n c) -> p n c", c=1024)
            oT = k.T([128, 8, 128], BF16, "oT")
            SCR_TOP = k.top
            qT = k.T([128, 8, T], BF16, "qT")
            kT = k.T([128, 2, Tk], BF16, "kT")
            vA = k.T([128, nkt, 2, 130], BF16, "vA")
            PT = k.T([128, nkt, CH], BF16, "PT")
            qf = k.T([128, D], F32, "qf"); qr = k.T([128, D], F32, "qr"); sq = k.T([128, D], F32, "sq")
            kvf = k.T([128, 512], F32, "kvf"); kr = k.T([128, 256], F32, "kr")
            qb = k.T([128, D], BF16, "qb"); kb = k.T([128, 256], BF16, "kb")
            ss = k.T([128, 16], F32, "ss"); rc = k.T([128, 1], F32, "rc")
            ropeT = k.T([128, ntile, 2, 64], F32, "ropeT") if nctx else None
            cst = k.T([128, 128], F32, "cst") if nctx else None
            cur_row = [None]
            for (t0, row, so, bidx) in seqs:
                if cur_row[0] != row:
                    cur_row[0] = row
                    get_mod(scT, LY, 1, row, plus1=True); get_mod(shT, LY, 0, row)
                k.V(lambda e: e.memset(vA.ap, 1.0), [], [vA])
                if nctx:
                    k.dma(ropeT.ap, rope.ap.rearrange("(n p) a c -> p n a c", p=128), [rope], [ropeT])
                if nctx:
                    for kvh in range(2):
                        for j in range(2):
                            k.dma(cst.ap, ck.ap[bidx, kvh, j * 128:(j + 1) * 128, :], [ck], [cst])
                            k.V(lambda e: e.tensor_copy(out=kb.ap[:, 0:128], in_=cst.ap), [cst], [kb])
                            p = nps(); pv = p.ap.bitcast(BF16)
                            k.tr(pv[:, 0:128], kb.ap[:, 0:128], identb.ap, [kb, identb], [p])
                            k.A(lambda e, pv=pv, kvh=kvh, j=j: e.copy(out=kT.ap[:, kvh, j * 128:(j + 1) * 128], in_=pv[:, 0:128]), [p], [kT])
                            k.dma(cst.ap, cv_.ap[bidx, kvh, j * 128:(j + 1) * 128, :], [cv_], [cst])
                            k.V(lambda e, kvh=kvh, j=j: e.tensor_copy(out=vA.ap[:, j, kvh, 0:128], in_=cst.ap), [cst], [vA])
                for ti in range(ntile):
                    xt = xts[ti % 2]
                    k.dma(xt.ap, X2.ap[(t0 + ti) * 128:(t0 + ti + 1) * 128, :], [X2], [xt])
                    k.V(lambda e, xt=xt: e.tensor_mul(out=tmpT.ap, in0=xt.ap, in1=scT.ap), [xt, scT], [tmpT])
                    k.V(lambda e: e.tensor_add(out=hb.ap, in0=tmpT.ap, in1=shT.ap), [tmpT, shT], [hb])
                    for half in range(2):
                        p = nps(); pv = p.ap.bitcast(BF16)
                        for kk in range(4):
                            kc = half * 4 + kk
                            k.tr(pv[:, kk * 128:(kk + 1) * 128], hb.ap[:, kc * 128:(kc + 1) * 128], identb.ap, [hb, identb], [p])
                        k.A(lambda e, pv=pv, half=half, ti=ti: e.copy(out=hT.ap[:, half * 4:half * 4 + 4, ti * 128:(ti + 1) * 128],
                                                                      in_=pv[:, 0:512].rearrange("p (a b) -> p a b", b=128)), [p], [hT])
                for ti in range(ntile):
                    tsl = slice(ti * 128, (ti + 1) * 128)
                    for half in range(2):
                        pq = nps()
                        for kk in range(8):
                            k.mm(pq.ap, hT.ap[:, kk, tsl], Wa.ap[:, kk, half * 512:(half + 1) * 512], kk == 0, kk == 7, [hT, Wa], [pq])
                        k.A(lambda e, pq=pq, half=half: e.copy(out=qf.ap[:, half * 512:(half + 1) * 512], in_=pq.ap), [pq], [qf])
                    pk = nps()
                    for kk in range(8):
                        k.mm(pk.ap, hT.ap[:, kk, tsl], Wa.ap[:, kk, 1024:1536], kk == 0, kk == 7, [hT, Wa], [pk])
                    k.A(lambda e, pk=pk: e.copy(out=kvf.ap, in_=pk.ap), [pk], [kvf])
                    k.V(lambda e: e.tensor_mul(out=sq.ap, in0=qf.ap, in1=qf.ap), [qf], [sq])
                    k.V(lambda e: e.tensor_reduce(out=ss.ap[:, 0:8], in_=sq.ap.rearrange("p (h d) -> p h d", d=128), axis=AX.X, op=ALU.add), [sq], [ss])
                    k.V(lambda e: e.tensor_mul(out=sq.ap[:, 0:256], in0=kvf.ap[:, 0:256], in1=kvf.ap[:, 0:256]), [kvf], [sq])
                    k.V(lambda e: e.tensor_reduce(out=ss.ap[:, 8:10], in_=sq.ap[:, 0:256].rearrange("p (h d) -> p h d", d=128), axis=AX.X, op=ALU.add), [sq], [ss])
                    k.V(lambda e: e.tensor_scalar(out=ss.ap[:, 0:10], in0=ss.ap[:, 0:10], scalar1=1.0 / 128.0, scalar2=EPS, op0=ALU.mult, op1=ALU.add), [ss], [ss])
                    k.A(lambda e: e.activation(out=ss.ap[:, 0:10], in_=ss.ap[:, 0:10], func=AF.Sqrt), [ss], [ss])
                    k.V(lambda e: e.reciprocal(out=ss.ap[:, 0:10], in_=ss.ap[:, 0:10]), [ss], [ss])
                    k.V(lambda e: e.tensor_mul(out=qf.ap.rearrange("p (h d) -> p h d", d=128), in0=qf.ap.rearrange("p (h d) -> p h d", d=128),
                                               in1=ss.ap[:, 0:8].unsqueeze(2).to_broadcast([128, 8, 128])), [qf, ss], [qf])
                    k.V(lambda e: e.tensor_mul(out=qf.ap.rearrange("p (h d) -> p h d", d=128), in0=qf.ap.rearrange("p (h d) -> p h d", d=128),
                                               in1=gqT.ap.unsqueeze(1).to_broadcast([128, 8, 128])), [qf, gqT], [qf])
                    k.V(lambda e: e.tensor_mul(out=kvf.ap[:, 0:256].rearrange("p (h d) -> p h d", d=128), in0=kvf.ap[:, 0:256].rearrange("p (h d) -> p h d", d=128),
                                               in1=ss.ap[:, 8:10].unsqueeze(2).to_broadcast([128, 2, 128])), [kvf, ss], [kvf])
                    k.V(lambda e: e.tensor_mul(out=kvf.ap[:, 0:256].rearrange("p (h d) -> p h d", d=128), in0=kvf.ap[:, 0:256].rearrange("p (h d) -> p h d", d=128),
                                               in1=gkT.ap.unsqueeze(1).to_broadcast([128, 2, 128])), [kvf, gkT], [kvf])
                    if so is not None:
                        for kvh in range(2):
                            k.dma(nk.ap[so, kvh, tsl, :], kvf.ap[:, kvh * 128:(kvh + 1) * 128], [kvf], [nk], is_out=True)
                            k.dma(nv.ap[so, kvh, tsl, :], kvf.ap[:, 256 + kvh * 128:256 + (kvh + 1) * 128], [kvf], [nv], is_out=True)
                    if nctx:
                        for (src, dst, H) in ((qf, qr, 8), (kvf, kr, 2)):
                            sv = src.ap[:, 0:H * 128].rearrange("p (h a f g) -> p h a f g", a=2, f=2, g=32)
                            dv = dst.ap[:, 0:H * 128].rearrange("p (h a f g) -> p h a f g", a=2, f=2, g=32)
                            tv = sq.ap[:, 0:H * 128].rearrange("p (h a f g) -> p h a f g", a=2, f=2, g=32)
                            cosb = ropeT.ap[:, ti, 0, :].rearrange("p (a g) -> p a g", g=32).unsqueeze(1).to_broadcast([128, H, 2, 32])
                            sinb = ropeT.ap[:, ti, 1, :].rearrange("p (a g) -> p a g", g=32).unsqueeze(1).to_broadcast([128, H, 2, 32])
                            x1 = sv[:, :, :, 0, :]; x2 = sv[:, :, :, 1, :]
                            k.V(lambda e, x1=x1, cosb=cosb, dv=dv: e.tensor_mul(out=dv[:, :, :, 0, :], in0=x1, in1=cosb), [src, ropeT], [dst])
                            k.V(lambda e, x2=x2, sinb=sinb, tv=tv: e.tensor_mul(out=tv[:, :, :, 0, :], in0=x2, in1=sinb), [src, ropeT], [sq])
                            k.V(lambda e, dv=dv, tv=tv: e.tensor_sub(out=dv[:, :, :, 0, :], in0=dv[:, :, :, 0, :], in1=tv[:, :, :, 0, :]), [dst, sq], [dst])
                            k.V(lambda e, x2=x2, cosb=cosb, dv=dv: e.tensor_mul(out=dv[:, :, :, 1, :], in0=x2, in1=cosb), [src, ropeT], [dst])
                            k.V(lambda e, x1=x1, sinb=sinb, tv=tv: e.tensor_mul(out=tv[:, :, :, 1, :], in0=x1, in1=sinb), [src, ropeT], [sq])
                            k.V(lambda e, dv=dv, tv=tv: e.tensor_add(out=dv[:, :, :, 1, :], in0=dv[:, :, :, 1, :], in1=tv[:, :, :, 1, :]), [dst, sq], [dst])
                        qsrc, ksrc = qr, kr
                    else:
                        qsrc, ksrc = qf, kvf
                    k.A(lambda e, qsrc=qsrc: e.copy(out=qb.ap, in_=qsrc.ap), [qsrc], [qb])
                    k.A(lambda e, ksrc=ksrc: e.copy(out=kb.ap, in_=ksrc.ap[:, 0:256]), [ksrc], [kb])
                    for kvh in range(2):
                        k.V(lambda e, kvh=kvh, ti=ti: e.tensor_copy(out=vA.ap[:, nctx // 128 + ti, kvh, 0:128], in_=kvf.ap[:, 256 + kvh * 128:256 + (kvh + 1) * 128]), [kvf], [vA])
                    p = nps(); pv = p.ap.bitcast(BF16)
                    for h in range(8):
                        k.tr(pv[:, h * 128:(h + 1) * 128], qb.ap[:, h * 128:(h + 1) * 128], identb.ap, [qb, identb], [p])
                    k.A(lambda e, pv=pv, tsl=tsl: e.copy(out=qT.ap[:, :, tsl], in_=pv[:, 0:1024].rearrange("p (a b) -> p a b", b=128)), [p], [qT])
                    p = nps(); pv = p.ap.bitcast(BF16)
                    for kvh in range(2):
                        k.tr(pv[:, kvh * 128:(kvh + 1) * 128], kb.ap[:, kvh * 128:(kvh + 1) * 128], identb.ap, [kb, identb], [p])
                    k.A(lambda e, pv=pv, ti=ti: e.copy(out=kT.ap[:, :, nctx + ti * 128:nctx + (ti + 1) * 128], in_=pv[:, 0:256].rearrange("p (a b) -> p a b", b=128)), [p], [kT])
                for h in range(8):
                    kvh = h // 4
                    for c0 in range(0, T, CH):
                        for kt in range(nkt):
                            ps_ = nps()
                            k.mm(ps_.ap[:, 0:CH], kT.ap[:, kvh, kt * 128:(kt + 1) * 128], qT.ap[:, h, c0:c0 + CH], True, True, [kT, qT], [ps_])
                            k.A(lambda e, ps_=ps_, kt=kt: e.activation(out=PT.ap[:, kt, :], in_=ps_.ap[:, 0:CH], func=AF.Exp, scale=128.0 ** -0.5), [ps_], [PT])
                        for tq in range(CH // 128):
                            po = nps()
                            for kt in range(nkt):
                                k.mm(po.ap[:, 0:129], PT.ap[:, kt, tq * 128:(tq + 1) * 128], vA.ap[:, kt, kvh, 0:129], kt == 0, kt == nkt - 1, [PT, vA], [po])
                            k.V(lambda e, po=po: e.reciprocal(out=rc.ap, in_=po.ap[:, 128:129]), [po], [rc])
                            tile_i = c0 // 128 + tq
                            k.V(lambda e, po=po, tile_i=tile_i, h=h: e.tensor_scalar(out=oS_ap[:, tile_i, h * 128:(h + 1) * 128], in0=po.ap[:, 0:128], scalar1=rc.ap[:, 0:1],
                                                                                      scalar2=None, op0=ALU.mult), [po, rc], [hT])
                k.barrier()
                k.top = SCR_TOP
                PP = alloc_post(row)
                for ti in range(ntile):
                    xt = xts[ti % 2]
                    k.dma(xt.ap, X2.ap[(t0 + ti) * 128:(t0 + ti + 1) * 128, :], [X2], [xt])
                    p = nps(); pv = p.ap.bitcast(BF16)
                    for kk in range(8):
                        k.tr(pv[:, kk * 128:(kk + 1) * 128], oS_ap[:, ti, kk * 128:(kk + 1) * 128], identb.ap, [hT, identb], [p])
                    k.A(lambda e, pv=pv: e.copy(out=oT.ap, in_=pv[:, 0:1024].rearrange("p (a b) -> p a b", b=128)), [p], [oT])
                    yps = []
                    for half in range(2):
                        py = nps()
                        for kk in range(8):
                            k.mm(py.ap, oT.ap[:, kk, :], Wao.ap[:, kk, half * 512:(half + 1) * 512], kk == 0, kk == 7, [oT, Wao], [py])
                        yps.append(py)
                    post_mixer(t0 + ti, xt, yps, PP["ga"], PP["lg"], PP["lb"], PP["sc2"], PP["sh2"], wrT, tmpT, h2bT, h2TT)
                k.barrier()

        k.top = BASE_TOP
        nseq = 32 if KSTOP != "A1" else 1
        run_seqs([(2 * s, 0, s, None) for s in range(nseq)], 256, 0)
        k.barrier()
        k.top = BASE_TOP
        run_seqs([(NTP + 8 * b, 1 + b, None, b) for b in range(2)], 1024, 256)
        k.barrier()
        k.top = PERSIST_TOP

    if KSTOP in ("A", "A1"):
        pass
    attn_layer()
    if KSTOP in ("A", "A1"):
        xt = k.T([128, D], F32)
        tl = list(range(NT)) if KSTOP == "A" else [0, 1] + list(range(64, 80))
        for t in tl:
            k.dma(xt.ap, X1.ap[t * 128:(t + 1) * 128, :], [X1], [xt])
            k.dma(yout.ap[t * 128:(t + 1) * 128, :], xt.ap, [xt], [yout], is_out=True)
        return k.finish()
    moe_layer(1, yout)

    return k.finish()


def _consts():
    s = np.arange(64)[:, None]
    t = np.arange(64)[None, :]
    c = -1.0 / 16.0
    cm = np.zeros((64, 6, 64), np.float32)
    cm[:, 0, :] = c * (s <= t)
    cm[:, 1, :] = c * (s > t)
    cm[:, 2, :] = c * (s >= t)
    cm[:, 3, :] = c * (s < t)
    cm[:, 4, :] = 1.0 * (t >= s)
    cm[:, 5, :] = 1.0 * (t <= s)
    return cm


def _rope_tables():
    T = 1024; W = 64
    rows = T // W
    row = np.repeat(np.arange(rows), W).astype(np.float32)
    col = np.tile(np.arange(W), rows).astype(np.float32)
    inv = (10000.0 ** (-np.arange(0, 64, 2, dtype=np.float32) / 64.0)).astype(np.float32)
    ang = np.stack([row[:, None] * inv, col[:, None] * inv], axis=1).astype(np.float32)
    cos = np.cos(ang).astype(np.float32).reshape(T, 64); sin = np.sin(ang).astype(np.float32).reshape(T, 64)
    return np.stack([cos, sin], axis=1).astype(np.float32)


def make_in_map(inp):
    f = lambda a: np.ascontiguousarray(np.asarray(a, dtype=np.float32))
    m = {}
    m["xin"] = f(np.concatenate([f(inp["x_prompt"]).reshape(-1, D), f(inp["x_sample"]).reshape(-1, D)], axis=0))
    cv = np.stack([f(inp["c_ctx"]), f(inp["c"])[0], f(inp["c"])[1]], axis=0)
    m["cvT"] = f(cv.T.reshape(8, 128, 3).transpose(1, 0, 2))
    m["w_mod"] = f(inp["w_mod"]); m["b_mod"] = f(inp["b_mod"])
    m["lng"] = f(inp["ln_g"]).reshape(4, D); m["lnb"] = f(inp["ln_b"]).reshape(4, D)
    wi = f(inp["w_gla_in"])[0]
    hd = lambda hh: np.concatenate([wi[:, hh * 128:(hh + 1) * 128], wi[:, 512 + hh * 128:512 + (hh + 1) * 128],
                                    wi[:, 1024 + hh * 256:1024 + (hh + 1) * 256], wi[:, 2048 + hh * 256:2048 + (hh + 1) * 256]], axis=1)
    m["w_in_hd"] = f(np.concatenate([hd(hh) for hh in range(4)], axis=1))
    m["w_g1"] = f(np.concatenate([f(inp["w_gla_gf1"])[0], f(inp["w_gla_gb1"])[0]], axis=1))
    m["w_gf2a"] = f(np.concatenate([f(inp["w_gla_gf2"])[0], f(inp["b_gla_gf"])[0][None, :]], axis=0))
    m["w_gb2a"] = f(np.concatenate([f(inp["w_gla_gb2"])[0], f(inp["b_gla_gb"])[0][None, :]], axis=0))
    m["gnorm"] = f(inp["g_gla_norm"])[0][None, :]
    m["w_out"] = f(inp["w_gla_out"])[0]
    m["S0f"] = f(inp["state_gla_fwd"])[:, 0]; m["S0b"] = f(inp["state_gla_bwd"])[:, 0]
    m["cmask"] = _consts(); m["identin"] = np.eye(128, dtype=np.float32)
    m["w_router"] = f(inp["w_router"])
    m["wg"] = f(inp["w_moe_gate"]); m["wu"] = f(inp["w_moe_up"]); m["wd"] = f(inp["w_moe_down"])
    io = np.zeros((128, 2), np.float32); io[:, 0] = np.arange(128)
    m["iota_in"] = io
    m["wa_in"] = f(inp["w_attn_in"])[0]; m["wa_out"] = f(inp["w_attn_out"])[0]
    m["gqk"] = f(np.stack([f(inp["g_attn_q"])[0], f(inp["g_attn_k"])[0]], axis=0))
    m["ck"] = f(inp["cache_attn_k"])[:, 0]; m["cv_"] = f(inp["cache_attn_v"])[:, 0]
    m["rope"] = _rope_tables()
    pp = np.arange(128)
    m["ustrict"] = (pp[:, None] < pp[None, :]).astype(np.float32)
    be = np.zeros((128, 2, 16), np.float32)
    be[:, 0, :] = np.arange(16)[None, :] * 1280.0
    be[:, 1, :] = np.arange(16)[None, :] * 1280.0 + 1024.0
    m["basein"] = be.reshape(128, 32)
    return m


_NC = None


def kernel(**inp):
    global _NC
    if _NC is None:
        _NC = build()
    m = make_in_map(inp)
    res = run_bass_kernel_spmd(_NC, [m], core_ids=[0])
    R = res.results[0]
    y = R["yout"]
    y_prompt = y[:8192].reshape(32, 256, D)
    y_sample = y[8192:].reshape(2, 1024, D)
    nf = R["nf"][:, None]; nb = R["nb"][:, None]
    nk = R["nk"][:, None]; nv = R["nv"][:, None]
    return tuple(np.ascontiguousarray(a, dtype=np.float32) for a in (y_prompt, y_sample, nf, nb, nk, nv))
```
